# Optimizing a Trainium2 kernel written in Bass

```python
import jax, jax.numpy as jnp
from jax import lax
import numpy as np

D_MODEL = 2048
BATCH = 4
SEQ = 4096
DEPTH = 4
DEC_BATCH = 32
DEC_SEQ = 32
PAST_LEN = 4096

CHUNK = 64
GM_CHUNK = 128
GM_GROUPS = 8
GM_WIDTH = 1024
GM_GROUP_DIM = GM_WIDTH // GM_GROUPS
LRU_WIDTH = 1024
LRU_BLOCKS = 8
LRU_BLOCK_DIM = LRU_WIDTH // LRU_BLOCKS
LRU_CONV = 4
LRU_C = 8.0
HG_HEADS = 8
HG_DK = 128
HG_DV = 128
HG_WIDTH = HG_HEADS * HG_DV
N_BRANCH = 3
MIX_WIDTH = GM_WIDTH + LRU_WIDTH + HG_WIDTH
IN_SPLITS = (GM_WIDTH, GM_WIDTH, GM_WIDTH,
             LRU_WIDTH, LRU_WIDTH,
             HG_HEADS * HG_DK, HG_HEADS * HG_DK,
             HG_WIDTH, HG_WIDTH,
             N_BRANCH * D_MODEL)
IN_WIDTH = sum(IN_SPLITS)
EPS = 1e-6

kernel_name = "hybrid_gmlp_rglru_hgrn2_stream_step"


def _split_points():
    pts, acc = [], 0
    for s in IN_SPLITS[:-1]:
        acc += s
        pts.append(acc)
    return pts


def rmsnorm(x, g):
    xf = x.astype(jnp.float32)
    y = xf * lax.rsqrt(jnp.mean(xf * xf, axis=-1, keepdims=True) + EPS)
    return (y * g.astype(jnp.float32)).astype(x.dtype)


def gmlp_mix(u, v, ws, bs):
    b, t, _ = v.shape
    L = min(t, GM_CHUNK)
    n = t // L
    mask = jnp.tril(jnp.ones((L, L), dtype=bool))
    w = jnp.where(mask[None], ws[:, :L, :L], 0.0)
    vg = v.reshape(b, n, L, GM_GROUPS, GM_GROUP_DIM)
    s = jnp.einsum('gts,bnsgc->bntgc', w, vg) + bs[:, :L].T[None, None, :, :, None]
    return u * s.reshape(b, t, GM_WIDTH)


def causal_conv(x, prev, w, bias):
    t = x.shape[1]
    xp = jnp.concatenate([prev.astype(x.dtype), x], axis=1)
    y = bias
    for j in range(LRU_CONV):
        y = y + w[j] * xp[:, j:j + t]
    return y, xp[:, -(LRU_CONV - 1):]


def block_diag(x, w, bias):
    b, t, _ = x.shape
    xb = x.reshape(b, t, LRU_BLOCKS, LRU_BLOCK_DIM)
    return jnp.einsum('btnd,nde->btne', xb, w).reshape(b, t, LRU_WIDTH) + bias


def rglru(x, h0, w_a, b_a, w_x, b_x, lam):
    f32 = jnp.float32
    gate_r = jax.nn.sigmoid(block_diag(x, w_a, b_a).astype(f32))
    gate_i = jax.nn.sigmoid(block_diag(x, w_x, b_x).astype(f32))
    log_a = -LRU_C * gate_r * jax.nn.softplus(-lam.astype(f32))
    a = jnp.exp(log_a)
    drive = jnp.sqrt(-jnp.expm1(2.0 * log_a)) * gate_i * x.astype(f32)
    drive = drive.at[:, 0].add(a[:, 0] * h0.astype(f32))

    def combine(lhs, rhs):
        al, bl = lhs
        ar, br = rhs
        return al * ar, ar * bl + br

    _, h = lax.associative_scan(combine, (a, drive), axis=1)
    return h.astype(x.dtype), h[:, -1].astype(h0.dtype)


def hgrn2_chunk(S, q, k, g, v):
    L = q.shape[1]
    G = jnp.cumsum(g, axis=1)
    o_inter = jnp.einsum('blhk,bhkv->blhv', q * jnp.exp(G), S)
    mask = jnp.tril(jnp.ones((L, L), dtype=bool))
    diff = G[:, :, None] - G[:, None, :]
    decay = jnp.exp(jnp.where(mask[None, :, :, None, None], diff, -jnp.inf))
    A = jnp.einsum('bthk,btshk,bshk->bhts', q, decay, k)
    o = o_inter + jnp.einsum('bhts,bshv->bthv', A, v)
    G_last = G[:, -1]
    k_dec = k * jnp.exp(G_last[:, None] - G)
    S_new = jnp.exp(G_last)[..., None] * S + jnp.einsum('blhk,blhv->bhkv', k_dec, v)
    return S_new, o


def hgrn2_scan(S0, q, k, g, v):
    b, t = q.shape[:2]
    L = min(t, CHUNK)
    n = t // L

    def to_chunks(a):
        return a.reshape(b, n, L, *a.shape[2:]).swapaxes(0, 1)

    S, o = lax.scan(lambda s, c: hgrn2_chunk(s, *c), S0.astype(jnp.float32),
                    (to_chunks(q), to_chunks(k), to_chunks(g), to_chunks(v)))
    return o.swapaxes(0, 1).reshape(b, t, HG_HEADS, HG_DV), S.astype(S0.dtype)


def hgrn2_lower_bounds(lb_param):
    p = jax.nn.softmax(lb_param.astype(jnp.float32), axis=0)
    cs = jnp.cumsum(p, axis=0)
    return cs - cs[0]


def mixer_layer(x, c, conv_prev, h0, S0, lb, w_ada, b_ada, norm_g, w_in, gm_vnorm_g, gm_ws, gm_bs,
                lru_conv_w, lru_conv_b, lru_wa, lru_ba, lru_wx, lru_bx, lru_lambda, hg_onorm_g,
                w_branch, w_out):
    b, t, _ = x.shape
    f32 = jnp.float32
    mod = jax.nn.silu(c) @ w_ada + b_ada
    shift, scale, gate = jnp.split(mod[:, None, :], 3, axis=-1)
    h = rmsnorm(x, norm_g) * (1.0 + scale) + shift
    z = h @ w_in
    au, av, ag, lx, lg, hq, hf, hi, hg, mg = jnp.split(z, _split_points(), axis=-1)

    au = jax.nn.gelu(au)
    av = rmsnorm(jax.nn.gelu(av), gm_vnorm_g)
    ya = gmlp_mix(au, av, gm_ws, gm_bs) * jax.nn.silu(ag)

    lx, conv_new = causal_conv(lx, conv_prev, lru_conv_w, lru_conv_b)
    hb, h_new = rglru(lx, h0, lru_wa, lru_ba, lru_wx, lru_bx, lru_lambda)
    yb = hb * jax.nn.silu(lg)

    q = jax.nn.silu(hq.astype(f32)).reshape(b, t, HG_HEADS, HG_DK)
    zf = hf.astype(f32).reshape(b, t, HG_HEADS, HG_DK)
    lbh = lb.reshape(HG_HEADS, HG_DK)
    log_f = jnp.logaddexp(jnp.log(lbh), jnp.log1p(-lbh) + jax.nn.log_sigmoid(zf))
    k = (1.0 - lbh) * jax.nn.sigmoid(-zf)
    v = hi.astype(f32).reshape(b, t, HG_HEADS, HG_DV)
    o, S_new = hgrn2_scan(S0, q, k, log_f, v)
    yc = rmsnorm(o, hg_onorm_g).astype(x.dtype).reshape(b, t, HG_WIDTH) * jax.nn.silu(hg)

    pa = ya @ w_branch[:GM_WIDTH]
    pb = yb @ w_branch[GM_WIDTH:GM_WIDTH + LRU_WIDTH]
    pc = yc @ w_branch[GM_WIDTH + LRU_WIDTH:]
    ga, gb, gc = jnp.split(jax.nn.sigmoid(mg), N_BRANCH, axis=-1)
    out = (ga * pa + gb * pb + gc * pc) @ w_out
    return x + gate * out, av, conv_new, h_new, S_new


def setup_inputs(seed: int = 0) -> dict:
    key = jax.random.key(seed)
    ks = jax.random.split(key, 32)
    nrm = lambda k, shape, s: jax.random.normal(k, shape, jnp.float32) * s
    a_c = 0.9 + 0.099 * jax.random.uniform(ks[20], (DEPTH, LRU_WIDTH), jnp.float32)
    a0 = a_c ** (1.0 / LRU_C)
    return {
        "x_prompt": nrm(ks[0], (BATCH, SEQ, D_MODEL), 1.0),
        "x_sample": nrm(ks[1], (DEC_BATCH, DEC_SEQ, D_MODEL), 1.0),
        "c_prompt": nrm(ks[2], (BATCH, D_MODEL), 1.0),
        "c_sample": nrm(ks[3], (DEC_BATCH, D_MODEL), 1.0),
        "state_rglru_conv": nrm(ks[4], (DEPTH, DEC_BATCH, LRU_CONV - 1, LRU_WIDTH), 1.0),
        "state_rglru_h": nrm(ks[5], (DEPTH, DEC_BATCH, LRU_WIDTH), 0.5),
        "state_hgrn2": nrm(ks[6], (DEPTH, DEC_BATCH, HG_HEADS, HG_DK, HG_DV), 0.3),
        "w_ada": nrm(ks[7], (DEPTH, D_MODEL, 3 * D_MODEL), D_MODEL ** -0.5),
        "b_ada": nrm(ks[8], (DEPTH, 3 * D_MODEL), 0.01),
        "norm_g": 1.0 + nrm(ks[9], (DEPTH, D_MODEL), 0.01),
        "w_in": nrm(ks[10], (DEPTH, D_MODEL, IN_WIDTH), D_MODEL ** -0.5),
        "gm_vnorm_g": 1.0 + nrm(ks[11], (DEPTH, GM_WIDTH), 0.01),
        "gm_ws": nrm(ks[12], (DEPTH, GM_GROUPS, GM_CHUNK, GM_CHUNK), GM_CHUNK ** -0.5),
        "gm_bs": 1.0 + nrm(ks[13], (DEPTH, GM_GROUPS, GM_CHUNK), 0.01),
        "lru_conv_w": nrm(ks[14], (DEPTH, LRU_CONV, LRU_WIDTH), LRU_CONV ** -0.5),
        "lru_conv_b": nrm(ks[15], (DEPTH, LRU_WIDTH), 0.01),
        "lru_wa": nrm(ks[16], (DEPTH, LRU_BLOCKS, LRU_BLOCK_DIM, LRU_BLOCK_DIM), LRU_BLOCK_DIM ** -0.5),
        "lru_ba": nrm(ks[17], (DEPTH, LRU_WIDTH), 0.01),
        "lru_wx": nrm(ks[18], (DEPTH, LRU_BLOCKS, LRU_BLOCK_DIM, LRU_BLOCK_DIM), LRU_BLOCK_DIM ** -0.5),
        "lru_bx": nrm(ks[19], (DEPTH, LRU_WIDTH), 0.01),
        "lru_lambda": jnp.log(a0) - jnp.log1p(-a0),
        "hg_lb": nrm(ks[21], (DEPTH, HG_HEADS * HG_DK), 0.1),
        "hg_onorm_g": 1.0 + nrm(ks[22], (DEPTH, HG_DV), 0.01),
        "w_branch": nrm(ks[23], (DEPTH, MIX_WIDTH, D_MODEL), GM_WIDTH ** -0.5),
        "w_out": nrm(ks[24], (DEPTH, D_MODEL, D_MODEL), D_MODEL ** -0.5),
        "final_g": 1.0 + nrm(ks[25], (D_MODEL,), 0.01),
    }


def reference(x_prompt, x_sample, c_prompt, c_sample, state_rglru_conv, state_rglru_h, state_hgrn2,
              w_ada, b_ada, norm_g, w_in, gm_vnorm_g, gm_ws, gm_bs, lru_conv_w, lru_conv_b,
              lru_wa, lru_ba, lru_wx, lru_bx, lru_lambda, hg_lb, hg_onorm_g, w_branch, w_out, final_g):
    lbs = hgrn2_lower_bounds(hg_lb)
    dt = x_prompt.dtype
    xp, xs = x_prompt, x_sample
    conv_p0 = jnp.zeros((BATCH, LRU_CONV - 1, LRU_WIDTH), dt)
    h_p0 = jnp.zeros((BATCH, LRU_WIDTH), dt)
    S_p0 = jnp.zeros((BATCH, HG_HEADS, HG_DK, HG_DV), dt)
    conv_p, h_p, S_p, conv_s, h_s, S_s, v_s = [], [], [], [], [], [], []
    for l in range(DEPTH):
        w = (w_ada[l], b_ada[l], norm_g[l], w_in[l], gm_vnorm_g[l], gm_ws[l], gm_bs[l],
             lru_conv_w[l], lru_conv_b[l], lru_wa[l], lru_ba[l], lru_wx[l], lru_bx[l],
             lru_lambda[l], hg_onorm_g[l], w_branch[l], w_out[l])
        xp, _, cp, hp, sp = mixer_layer(xp, c_prompt, conv_p0, h_p0, S_p0, lbs[l], *w)
        xs, vs, cs, hs, ss = mixer_layer(xs, c_sample, state_rglru_conv[l], state_rglru_h[l],
                                         state_hgrn2[l], lbs[l], *w)
        conv_p.append(cp); h_p.append(hp); S_p.append(sp)
        conv_s.append(cs); h_s.append(hs); S_s.append(ss); v_s.append(vs)
    y_prompt = rmsnorm(xp, final_g)
    y_sample = rmsnorm(xs, final_g)
    return (y_prompt, y_sample, jnp.stack(conv_p), jnp.stack(h_p), jnp.stack(S_p),
            jnp.stack(conv_s), jnp.stack(h_s), jnp.stack(S_s), jnp.stack(v_s))
```

```python
import numpy as np
from contextlib import ExitStack
import concourse.bass as bass
import concourse.mybir as mybir
from concourse.bass_utils import run_bass_kernel_spmd

F32 = mybir.dt.float32
BF16 = mybir.dt.bfloat16
AF = mybir.ActivationFunctionType
ALU = mybir.AluOpType

D = 2048
KC = 16
DEPTH = 4
SEQ = 4096
NTP = 512
NSMP = 128
IN_W = 15360
EPS = 1e-6
OFF = dict(au=0, av=1024, ag=2048, lx=3072, lg=4096, hq=5120, hf=6144, hi=7168, hg=8192, mg=9216)
ENGS = ("pe", "act", "dve", "pool", "sp")
USE_GELU_TANH = True


def vec_layout(depth):
    lay = {}
    o = 0
    for name, n in (("c", 16 * 5), ("bada", depth * 48), ("normg", depth * 16), ("finalg", 16),
                    ("ba", depth * 8), ("bx", depth * 8), ("lam", depth * 8), ("convb", depth * 8),
                    ("convw", depth * 4 * 8), ("hglb", depth * 8),
                    ("convs", depth * 8 * 4 * 3), ("hs", depth * 8 * 4)):
        lay[name] = (o, n)
        o += n
    return lay, o


def const_layout(depth):
    lay = {}
    o = 0
    for name, n in (("ident", 128), ("triu", 128), ("rmask_p", 512), ("rmask_s", 128),
                    ("onormb", depth * 128), ("ones", 128), ("maskP", 512), ("maskS", 128)):
        lay[name] = (o, n)
        o += n
    return lay, o


SO_LAYOUT = None


def so_layout(depth):
    lay = {}
    o = 0
    for name, n in (("convp", depth * 8 * 3), ("hp", depth * 8), ("convs", depth * 8 * 4 * 3), ("hs", depth * 8 * 4)):
        lay[name] = (o, n)
        o += n
    return lay, o


class Prog:
    def __init__(self):
        self.lists = {e: [] for e in ENGS}
        self.cnt = {e: 0 for e in ENGS}
        self.known = {e: {} for e in ENGS}
        self.lastw = {}
        self.readers = {}
        self.dmacnt = {}
        self.dmasems = []

    def _deps(self, eng, reads, writes):
        need = {}

        def add(t):
            if t is None:
                return
            s, v = t
            if need.get(s, 0) < v:
                need[s] = v
        for k in reads:
            add(self.lastw.get(k))
        for k in writes:
            add(self.lastw.get(k))
            for s, v in self.readers.get(k, {}).items():
                add((s, v))
        kn = self.known[eng]
        for s, v in need.items():
            if kn.get(s, 0) < v:
                self.lists[eng].append(("w", s, v))
                kn[s] = v

    def _commit(self, tok, reads, writes):
        s, v = tok
        for k in writes:
            self.lastw[k] = tok
            self.readers[k] = {}
        for k in reads:
            r = self.readers.setdefault(k, {})
            if r.get(s, 0) < v:
                r[s] = v

    def op(self, eng, fn, reads=(), writes=(), inc=True):
        self._deps(eng, reads, writes)
        if inc:
            self.cnt[eng] += 1
            tok = (eng, self.cnt[eng])
            self.lists[eng].append(("i", fn, eng, 1))
        else:
            tok = (eng, self.cnt[eng] + 1)
            self.lists[eng].append(("i", fn, None, 0))
        self._commit(tok, reads, writes)

    def dma(self, q, fn, sem, reads=(), writes=()):
        self._deps(q, reads, writes)
        if sem not in self.dmacnt:
            self.dmacnt[sem] = 0
            self.dmasems.append(sem)
        self.dmacnt[sem] += 16
        tok = (sem, self.dmacnt[sem])
        self.lists[q].append(("i", fn, sem, 16))
        self._commit(tok, reads, writes)

    def fence(self, oldkeys, newkeys):
        acc = {}
        for k in oldkeys:
            t = self.lastw.get(k)
            if t is not None and acc.get(t[0], 0) < t[1]:
                acc[t[0]] = t[1]
            for s, v in self.readers.get(k, {}).items():
                if acc.get(s, 0) < v:
                    acc[s] = v
        for k in newkeys:
            r = self.readers.setdefault(k, {})
            for s, v in acc.items():
                if r.get(s, 0) < v:
                    r[s] = v

    def final_wait(self, eng, sems):
        for s in sems:
            v = self.dmacnt.get(s, 0)
            if v:
                self.lists[eng].append(("w", s, v))


def tile_cfg(kind):
    if kind == "p":
        NT = NTP
        blocks = [(i * 128, 128) for i in range(4)]
        segs = [(0, NT, 0)]
        L = 32
        chunks = []
        for i in range(NT // L):
            chunks.append(dict(blk=i // 2, p0=(i % 2) * 32, ln=L, off=i * L, seg=0, ci=i))
        hblocks = [(i * 64, 64) for i in range(8)]
        return dict(kind="p", NT=NT, blocks=blocks, hblocks=hblocks, segs=segs, L=L, chunks=chunks, rmask="rmask_p")
    NT = NSMP
    blocks = [(i * 32, 32) for i in range(4)]
    segs = [(i * 32, 32, 1 + i) for i in range(4)]
    chunks = [dict(blk=i, p0=0, ln=32, off=i * 32, seg=i, ci=i) for i in range(4)]
    return dict(kind="s", NT=NT, blocks=blocks, hblocks=blocks, segs=segs, L=32, chunks=chunks, rmask="rmask_s")


def build(depth=DEPTH, n_pt=SEQ // NTP, debug=False):
    SEQL = n_pt * NTP
    VL, NV = vec_layout(depth)
    CL, NCONST = const_layout(depth)
    SL, NSO = so_layout(depth)
    nc = bass.Bass("TRN2", target_bir_lowering=False)

    def din(name, shape):
        return nc.dram_tensor(name, shape, F32, kind="ExternalInput").ap()

    def dout(name, shape):
        return nc.dram_tensor(name, shape, F32, kind="ExternalOutput").ap()

    xpT = din("xpT", [D, SEQL])
    xsT = din("xsT", [D, NSMP])
    vecs = din("vecs", [128, NV])
    consts = din("consts", [128, NCONST])
    Ssin = din("Ssin", [depth, 4, 128, 1024])
    w_ada = din("w_ada", [depth, D, 3 * D])
    w_in = din("w_in", [depth, D, IN_W])
    w_br = din("w_branch", [depth, 3072, D])
    w_out = din("w_out", [depth, D, D])
    gmw = din("gmw", [depth, 128, 1024])
    gmb = din("gmb", [depth, 1, 1024])
    vnb = din("vnb", [depth, 128, 1024])
    lwa = din("lwa", [depth, 128, 1024])
    lwx = din("lwx", [depth, 128, 1024])

    yT = dout("yT", [D, SEQL + NSMP])
    small_out = dout("small_out", [128, NSO])
    Sp_out = dout("Sp_out", [depth, 128, 1024])
    Ss_out = dout("Ss_out", [depth, 4, 128, 1024])
    vs_out = dout("vs_out", [depth, 4, 32, 1024])
    dbg = dout("dbg", [2, 4, 128, 8 * NTP]) if debug else None

    P = Prog()
    es = ExitStack()

    def sb(name, shape, dt=F32):
        return es.enter_context(nc.sbuf_tensor(name, shape, dt))

    xT = sb("xT", [128, KC, NTP])
    hT = sb("hT", [128, KC, NTP], BF16)
    yA = sb("yA", [128, 8, NTP], BF16)
    yB = sb("yB", [128, 8, NTP], BF16)
    yC = sb("yC", [128, 8, NTP], BF16)
    NRING = 3
    ring = [sb(f"ring{i}", [128, 4096], BF16) for i in range(NRING)]
    F1 = sb("F1", [128, 4224])
    F2 = sb("F2", [128, 2048])
    B1 = sb("B1", [128, 4096], BF16)
    B2 = sb("B2", [128, 4096], BF16)
    B3 = sb("B3", [128, 4096], BF16)
    Sst = sb("Sst", [128, depth, 1024])
    rstd = sb("rstd", [128, NTP])
    vec_sb = sb("vec_sb", [128, NV])
    const_sb = sb("const_sb", [128, NCONST])
    ident_bf = sb("ident_bf", [128, 128], BF16)
    ones_bf = sb("ones_bf", [128, 128], BF16)
    rmask_bf = sb("rmask_bf", [128, NTP + NSMP], BF16)
    scT = sb("scT", [128, 16, 5], BF16)
    modT = sb("modT", [128, depth, 48, 5])
    Amod = sb("Amod", [128, depth, 16, 5])
    lbs = sb("lbs", [128, depth, 8])
    oml = sb("oml", [128, depth, 8])
    s1 = sb("s1", [128, depth, 8])
    s2 = sb("s2", [128, depth, 8])
    sm = sb("sm", [128, 512])
    gmw_bf = sb("gmw_bf", [128, 1024], BF16)
    gmb_sb = sb("gmb_sb", [1, 1024])
    vnb_sb = sb("vnb_sb", [128, 1024])
    wa_bf = sb("wa_bf", [128, 1024], BF16)
    wx_bf = sb("wx_bf", [128, 1024], BF16)
    convP = sb("convP", [128, depth, 8, 3])
    hP = sb("hP", [128, depth, 8])
    so_sb = sb("so_sb", [128, NSO])
    hgs = sb("hgs", [128, 5, 2, 16])
    ssq = sb("ssq", [128, 16])
    rsq = sb("rsq", [128, 16])

    NPS = 8
    psb = [es.enter_context(nc.psum_tensor(f"ps{i}", [128, 512], F32)) for i in range(NPS)]
    ps_i = [0]

    def newps():
        i = ps_i[0]
        ps_i[0] = (i + 1) % NPS
        return psb[i], f"ps{i}"

    def vcol(name, idx=0, n=1):
        o, _ = VL[name]
        return vec_sb[:, o + idx:o + idx + n]

    def ccol(name, idx=0, n=1):
        o, _ = CL[name]
        return const_sb[:, o + idx:o + idx + n]

    def act(out, in_, func, reads, writes, bias=None, scale=None, accum_out=None):
        kw = {}
        if bias is not None:
            kw["bias"] = bias
        if scale is not None:
            kw["scale"] = scale
        if accum_out is not None:
            kw["accum_out"] = accum_out
        P.op("act", lambda e: e.activation(out=out, in_=in_, func=func, **kw), reads, writes)

    def tt(out, in0, in1, op, reads, writes, eng="dve"):
        P.op(eng, lambda e: e.tensor_tensor(out=out, in0=in0, in1=in1, op=op), reads, writes)

    def ts(out, in0, s1_, s2_, op0, op1, reads, writes, eng="dve"):
        if op1 is None:
            P.op(eng, lambda e: e.tensor_scalar(out=out, in0=in0, scalar1=s1_, scalar2=None, op0=op0), reads, writes)
        else:
            P.op(eng, lambda e: e.tensor_scalar(out=out, in0=in0, scalar1=s1_, scalar2=s2_, op0=op0, op1=op1), reads, writes)

    def stt(out, in0, scalar, in1, op0, op1, reads, writes):
        P.op("dve", lambda e: e.scalar_tensor_tensor(out=out, in0=in0, scalar=scalar, in1=in1, op0=op0, op1=op1), reads, writes)

    def vcopy(out, in_, reads, writes, eng="dve"):
        P.op(eng, lambda e: e.tensor_copy(out=out, in_=in_), reads, writes)

    def mm(out, lhsT, rhs, start, stop, reads, writes, touch=()):
        P.op("pe", lambda e: e.matmul(out, lhsT, rhs, start=start, stop=stop), reads, writes, inc=stop)
        for k in touch:
            P.lastw[k] = ("pe", P.cnt["pe"])

    def transpose(out, in_, ident, reads, writes):
        P.op("pe", lambda e: e.transpose(out, in_, ident), reads, writes)

    def dma(q, out, in_, sem, reads, writes):
        P.dma(q, lambda e: e.dma_start(out=out, in_=in_), sem, reads, writes)

    ring_i = [0]

    def load_w(src, kcn, ncn):
        i = ring_i[0]
        ring_i[0] = (i + 1) % NRING
        view = ring[i][:, 0:kcn * ncn].rearrange("p (k n) -> p k n", k=kcn)
        dma("pool", view, src, f"wsem{i}", [], [f"ring{i}"])
        return view, f"ring{i}"

    def w_in_src(l, c0, ncn=256):
        return w_in[l, :, c0:c0 + ncn].rearrange("(k p) n -> p k n", p=128)

    dma("sp", vec_sb[:], vecs[:, :], "c0", [], ["vec"])
    dma("sp", const_sb[:], consts[:, :], "c1", [], ["const"])
    vcopy(ident_bf[:], ccol("ident", 0, 128), ["const"], ["ident_bf"])
    vcopy(ones_bf[:], ccol("ones", 0, 128), ["const"], ["ones_bf"])
    vcopy(rmask_bf[:, 0:NTP], ccol("rmask_p", 0, NTP), ["const"], ["rmask"])
    vcopy(rmask_bf[:, NTP:NTP + NSMP], ccol("rmask_s", 0, NSMP), ["const"], ["rmask"])
    act(scT[:].rearrange("p k s -> p (k s)"), vcol("c", 0, 80), AF.Silu, ["vec"], ["scT"])
    triu_u32 = const_sb[:, CL["triu"][0]:CL["triu"][0] + 128].bitcast(mybir.dt.uint32)
    maskP_u32 = const_sb[:, CL["maskP"][0]:CL["maskP"][0] + 512].bitcast(mybir.dt.uint32)
    maskS_u32 = const_sb[:, CL["maskS"][0]:CL["maskS"][0] + 128].bitcast(mybir.dt.uint32)
    P.op("dve", lambda e: e.memset(B3[:, 1024:2048], 0.0), [], ["AmA0", "AmA1"])
    P.op("dve", lambda e: e.memset(Sst[:].rearrange("p l n -> p (l n)"), 0.0), [], [f"S{l}_{hd}" for l in range(depth) for hd in range(8)])
    P.op("dve", lambda e: e.memset(convP[:].rearrange("p l c t -> p (l c t)"), 0.0), [], ["convP"])
    P.op("dve", lambda e: e.memset(hP[:].rearrange("p l c -> p (l c)"), 0.0), [], ["hP"])
    ex = sm[:, 0:depth * 8].rearrange("p (l c) -> p l c", l=depth)
    act(sm[:, 0:depth * 8], vcol("hglb", 0, depth * 8), AF.Exp, ["vec"], ["sm"])
    ssum = sm[:, 64:72]
    vcopy(ssum, ex[:, 0, :], ["sm"], ["sm"])
    for l in range(1, depth):
        tt(ssum, ssum, ex[:, l, :], ALU.add, ["sm"], ["sm"])
    P.op("dve", lambda e: e.reciprocal(out=sm[:, 72:80], in_=ssum), ["sm"], ["sm"])
    P.op("dve", lambda e: e.memset(lbs[:, 0, :], 0.0), [], ["lbs"])
    for l in range(1, depth):
        tt(sm[:, 80:88], ex[:, l, :], sm[:, 72:80], ALU.mult, ["sm"], ["sm"])
        tt(lbs[:, l, :], lbs[:, l - 1, :], sm[:, 80:88], ALU.add, ["sm", "lbs"], ["lbs"])
    lbf = lbs[:].rearrange("p l c -> p (l c)")
    ts(oml[:].rearrange("p l c -> p (l c)"), lbf, -1.0, 1.0, ALU.mult, ALU.add, ["lbs"], ["oml"])
    act(sm[:, 128:128 + depth * 8], vcol("lam", 0, depth * 8), AF.Exp, ["vec"], ["sm"], scale=-1.0)
    act(sm[:, 192:192 + depth * 8], sm[:, 128:128 + depth * 8], AF.Ln, ["sm"], ["sm"], bias=1.0)
    ts(s1[:].rearrange("p l c -> p (l c)"), sm[:, 192:192 + depth * 8], -8.0, None, ALU.mult, None, ["sm"], ["s1"])
    ts(s2[:].rearrange("p l c -> p (l c)"), sm[:, 192:192 + depth * 8], -16.0, None, ALU.mult, None, ["sm"], ["s2"])

    for l in range(depth):
        for blk in range(24):
            wv, wk = load_w(w_ada[l, :, blk * 256:(blk + 1) * 256].rearrange("(k p) n -> p k n", p=128), 16, 256)
            for fc in range(2):
                ps, pk = newps()
                ch = blk * 2 + fc
                for kc in range(KC):
                    mm(ps[:, 0:5], wv[:, kc, fc * 128:(fc + 1) * 128], scT[:, kc, :], kc == 0, kc == KC - 1,
                       [wk, "scT"], [pk] if kc == 0 else [])
                ts(modT[:, l, ch, :], ps[:, 0:5], vcol("bada", l * 48 + ch), None, ALU.add, None, [pk, "vec"], ["modT"])
        ts(sm[:, 256:256 + 80], modT[:, l, 16:32, :].rearrange("p k s -> p (k s)"), 1.0, None, ALU.add, None, ["modT"], ["sm"])
        smv = sm[:, 256:256 + 80].rearrange("p (k s) -> p k s", k=16)
        for s in range(5):
            tt(Amod[:, l, :, s], smv[:, :, s], vcol("normg", l * 16, 16), ALU.mult, ["sm", "vec"], ["Amod"])

    def rms_stats(NT):
        for kc in range(KC):
            act(hT[:, kc, 0:NT], xT[:, kc, 0:NT], AF.Square, [f"xT{kc}"], [f"hT{kc}"])
        ps, pk = newps()
        for kc in range(KC):
            mm(ps[:, 0:NT], ones_bf[:], hT[:, kc, 0:NT], kc == 0, kc == KC - 1, ["ones_bf", f"hT{kc}"], [pk] if kc == 0 else [])
        act(rstd[:, 0:NT], ps[:, 0:NT], AF.Ln, [pk], ["rstd"], bias=EPS, scale=1.0 / D)
        act(rstd[:, 0:NT], rstd[:, 0:NT], AF.Exp, ["rstd"], ["rstd"], scale=-0.5)

    def layer(tc, l, last_tile):
        NT = tc["NT"]
        blocks = tc["blocks"]
        segs = tc["segs"]
        nb = len(blocks)
        is_s = tc["kind"] == "s"
        dma("pool", gmw_bf[:], gmw[l], "lc0", [], ["gmw_bf"])
        dma("pool", wa_bf[:], lwa[l], "lc1", [], ["wa_bf"])
        dma("pool", wx_bf[:], lwx[l], "lc2", [], ["wx_bf"])
        dma("sp", gmb_sb[:], gmb[l], "lc3", [], ["gmb_sb"])
        dma("sp", vnb_sb[:], vnb[l], "lc4", [], ["vnb_sb"])
        for g in range(8):
            tt(gmw_bf[:, g * 128:(g + 1) * 128], gmw_bf[:, g * 128:(g + 1) * 128], ccol("triu", 0, 128), ALU.mult,
               ["gmw_bf", "const"], ["gmw_bf"])
        rms_stats(NT)
        Tn = [F1[:, i * 512:(i + 1) * 512] for i in range(2)]
        P.fence(["F1"], ["Tn0", "Tn1"])
        for kc in range(KC):
            for (so, sl, sq) in segs:
                t = Tn[kc % 2]
                stt(t[:, so:so + sl], xT[:, kc, so:so + sl], Amod[:, l, kc, sq:sq + 1], rstd[:, so:so + sl], ALU.mult, ALU.mult,
                    [f"xT{kc}", "Amod", "rstd"], [f"Tn{kc % 2}"])
                act(hT[:, kc, so:so + sl], t[:, so:so + sl], AF.Identity, [f"Tn{kc % 2}", "modT"], [f"hT{kc}"],
                    bias=modT[:, l, kc, sq:sq + 1], scale=1.0)
        P.fence(["Tn0", "Tn1"], ["F1"])
        hkeys = [f"hT{kc}" for kc in range(KC)]

        def fm_group(wv, wk, c0, kcn, rhs, rkeys, n0=0, n1=None):
            n1 = NT if n1 is None else n1
            ps, pk = newps()
            for kc in range(kcn):
                mm(ps[:, n0:n1], wv[:, kc, c0:c0 + 128], rhs[:, kc, n0:n1], kc == 0, kc == kcn - 1,
                   [wk] + rkeys, [pk] if kc == 0 else [])
            return ps, pk

        def tm_group(wv, wk, off, ln, ncn):
            ps, pk = newps()
            for kc in range(KC):
                mm(ps[0:ln, 0:ncn], hT[:, kc, off:off + ln], wv[:, kc, 0:ncn], kc == 0, kc == KC - 1,
                   [wk] + hkeys, [pk] if kc == 0 else [])
            return ps, pk

        gvt = F1[:, 0:nb * 1024].rearrange("p (b n) -> p b n", b=nb)
        vn = B1[:, 0:nb * 1024].rearrange("p (b n) -> p b n", b=nb)
        p1 = B2[:, 0:8 * NTP].rearrange("p (c n) -> p c n", c=8)
        for s4 in range(4):
            wv, wk = load_w(w_in_src(l, OFF["av"] + s4 * 256), 16, 256)
            for b, (off, ln) in enumerate(blocks):
                ps, pk = tm_group(wv, wk, off, ln, 256)
                if USE_GELU_TANH:
                    act(gvt[0:ln, b, s4 * 256:(s4 + 1) * 256], ps[0:ln, 0:256], AF.Gelu_apprx_tanh, [pk], ["F1"])
                else:
                    gelu_compose(gvt[0:ln, b, s4 * 256:(s4 + 1) * 256], ps[0:ln, 0:256], pk, "F1", ln, 256)
        ln0 = blocks[0][1]
        for b, (off, ln) in enumerate(blocks):
            act(B3[0:ln, 0:1024], gvt[0:ln, b, :], AF.Square, ["F1"], ["B3", "ssq"], accum_out=ssq[0:ln, b:b + 1])
        act(rsq[0:ln0, 0:nb], ssq[0:ln0, 0:nb], AF.Ln, ["ssq"], ["rsq"], bias=EPS, scale=1.0 / 1024)
        act(rsq[0:ln0, 0:nb], rsq[0:ln0, 0:nb], AF.Exp, ["rsq"], ["rsq"], scale=-0.5)
        for b, (off, ln) in enumerate(blocks):
            if is_s:
                stt(gvt[0:ln, b, :], gvt[0:ln, b, :], rsq[0:ln, b:b + 1], vnb_sb[0:ln, :], ALU.mult, ALU.mult,
                    ["F1", "rsq", "vnb_sb"], ["F1"])
                vcopy(vn[0:ln, b, :], gvt[0:ln, b, :], ["F1"], ["B1"])
                dma("sp", vs_out[l, b], gvt[0:ln, b, :], "ostv", ["F1"], [])
            else:
                stt(vn[0:ln, b, :], gvt[0:ln, b, :], rsq[0:ln, b:b + 1], vnb_sb[0:ln, :], ALU.mult, ALU.mult,
                    ["F1", "rsq", "vnb_sb"], ["B1"])
        tA = F2[:, 0:512]
        tB = F2[:, 512:1024]
        for s4 in range(4):
            wvu, wku = load_w(w_in_src(l, OFF["au"] + s4 * 256), 16, 256)
            wvg, wkg = load_w(w_in_src(l, OFF["ag"] + s4 * 256), 16, 256)
            for fc in range(2):
                c = s4 * 2 + fc
                psu, pku = fm_group(wvu, wku, fc * 128, 16, hT, hkeys)
                psg, pkg = fm_group(wvg, wkg, fc * 128, 16, hT, hkeys)
                if USE_GELU_TANH:
                    act(tA[:, 0:NT], psu[:, 0:NT], AF.Gelu_apprx_tanh, [pku], ["F2a"])
                else:
                    gelu_compose(tA[:, 0:NT], psu[:, 0:NT], pku, "F2a", 128, NT)
                act(tB[:, 0:NT], psg[:, 0:NT], AF.Silu, [pkg], ["F2b"])
                tt(p1[:, c, 0:NT], tA[:, 0:NT], tB[:, 0:NT], ALU.mult, ["F2a", "F2b"], ["B2"])
        for g in range(8):
            ps, pk = newps()
            for b, (off, ln) in enumerate(blocks):
                mm(ps[:, off:off + ln], vn[0:ln, b, g * 128:(g + 1) * 128], gmw_bf[0:ln, g * 128:g * 128 + ln], True, False,
                   ["B1", "gmw_bf"], [pk] if b == 0 else [])
                mm(ps[:, off:off + ln], ccol("ones", 0, 128)[0:1, :], gmb_sb[0:1, g * 128:g * 128 + ln], False, True,
                   ["const", "gmb_sb"], [], touch=[pk])
            tt(yA[:, g, 0:NT], p1[:, g, 0:NT], ps[:, 0:NT], ALU.mult, ["B2", pk], ["yA"])

        NTH = NT + 3 * len(segs)
        P.fence(["F1", "F2a", "F2b"], [f"TB{i}" for i in range(10)])
        P.fence(["B1"], ["slg0", "slg1"])
        TB = [F1[:, i * 528:(i + 1) * 528] for i in range(8)] + [F2[:, 1024:1536], F2[:, 1536:2048]]
        for s4 in range(4):
            wvx, wkx = load_w(w_in_src(l, OFF["lx"] + s4 * 256), 16, 256)
            wvg, wkg = load_w(w_in_src(l, OFF["lg"] + s4 * 256), 16, 256)
            psx_l = [fm_group(wvx, wkx, fc * 128, 16, hT, hkeys) for fc in range(2)]
            psg_l = [fm_group(wvg, wkg, fc * 128, 16, hT, hkeys) for fc in range(2)]

            def chunk_gen(fc, s4=s4, psx_l=psx_l, psg_l=psg_l):
                c = s4 * 2 + fc
                par = (c % 2) * 5
                lxh, kx = TB[par + 0], f"TB{par + 0}"
                xc, kxc = TB[par + 1], f"TB{par + 1}"
                tr, ktr = TB[par + 2], f"TB{par + 2}"
                ti, kti = TB[par + 3], f"TB{par + 3}"
                ta, kta = TB[par + 4], f"TB{par + 4}"
                xcb = B3[:, (c % 2) * 512:(c % 2) * 512 + 512]
                kxb = f"B3x{c % 2}"
                slg = B1[:, (c % 2) * 512:(c % 2) * 512 + 512]
                kslg = f"slg{c % 2}"
                psx, pkx = psx_l[fc]
                psg, pkg = psg_l[fc]
                lxv = lxh[:, 0:NTH].rearrange("p (s n) -> p s n", s=len(segs))
                for si, (so, sl, sq) in enumerate(segs):
                    if is_s:
                        o = VL["convs"][0] + ((l * 8 + c) * 4 + si) * 3
                        vcopy(lxv[:, si, 0:3], vec_sb[:, o:o + 3], ["vec"], [kx])
                    else:
                        vcopy(lxv[:, si, 0:3], convP[:, l, c, :], ["convP"], [kx])
                    act(lxv[:, si, 3:3 + sl], psx[:, so:so + sl], AF.Copy, [pkx], [kx])
                act(slg[:, 0:NT], psg[:, 0:NT], AF.Silu, [pkg], [kslg])
                for si, (so, sl, sq) in enumerate(segs):
                    if is_s:
                        o = SL["convs"][0] + ((l * 8 + c) * 4 + si) * 3
                        vcopy(so_sb[:, o:o + 3], lxv[:, si, sl:sl + 3], [kx], ["so_sb"])
                    else:
                        vcopy(convP[:, l, c, :], lxv[:, si, sl:sl + 3], [kx], ["convP"])
                yield
                sl = segs[0][1]
                xcv = xc[:, 0:NT].rearrange("p (s n) -> p s n", s=len(segs))
                cw = lambda j: vcol("convw", (l * 4 + j) * 8 + c)
                ts(xcv, lxv[:, :, 0:sl], cw(0), vcol("convb", l * 8 + c), ALU.mult, ALU.add, [kx, "vec"], [kxc])
                for j in range(1, 4):
                    stt(xcv, lxv[:, :, j:j + sl], cw(j), xcv, ALU.mult, ALU.add, [kx, kxc, "vec"], [kxc])
                vcopy(xcb[:, 0:NT], xc[:, 0:NT], [kxc], [kxb])
                yield
                psr, pkr = newps()
                mm(psr[:, 0:NT], wa_bf[:, c * 128:(c + 1) * 128], xcb[:, 0:NT], True, True, ["wa_bf", kxb], [pkr])
                psi, pki = newps()
                mm(psi[:, 0:NT], wx_bf[:, c * 128:(c + 1) * 128], xcb[:, 0:NT], True, True, ["wx_bf", kxb], [pki])
                act(tr[:, 0:NT], psr[:, 0:NT], AF.Sigmoid, [pkr, "vec"], [ktr], bias=vcol("ba", l * 8 + c), scale=1.0)
                act(ti[:, 0:NT], psi[:, 0:NT], AF.Sigmoid, [pki, "vec"], [kti], bias=vcol("bx", l * 8 + c), scale=1.0)
                yield
                act(ta[:, 0:NT], tr[:, 0:NT], AF.Exp, [ktr, "s1"], [kta], scale=s1[:, l, c:c + 1])
                act(lxh[:, 0:NT], tr[:, 0:NT], AF.Tanh, [ktr, "s1"], [kx], scale=s1[:, l, c:c + 1])
                act(tr[:, 0:NT], tr[:, 0:NT], AF.Exp, [ktr, "s2"], [ktr], scale=s2[:, l, c:c + 1])
                yield
                stt(tr[:, 0:NT], tr[:, 0:NT], 1.0, lxh[:, 0:NT], ALU.add, ALU.mult, [ktr, kx], [ktr])
                tt(ti[:, 0:NT], ti[:, 0:NT], xc[:, 0:NT], ALU.mult, [kti, kxc], [kti])
                yield
                act(tr[:, 0:NT], tr[:, 0:NT], AF.Ln, [ktr], [ktr], scale=-1.0)
                act(tr[:, 0:NT], tr[:, 0:NT], AF.Exp, [ktr], [ktr], scale=0.5)
                yield
                tt(ti[:, 0:NT], ti[:, 0:NT], tr[:, 0:NT], ALU.mult, [kti, ktr], [kti])
                for si, (so, sl, sq) in enumerate(segs):
                    if is_s:
                        o = VL["hs"][0] + (l * 8 + c) * 4 + si
                        init = vec_sb[:, o:o + 1]
                        ik = "vec"
                    else:
                        init = hP[:, l, c:c + 1]
                        ik = "hP"
                    P.op("dve", lambda e, so=so, sl=sl, init=init, tr=tr, ta=ta, ti=ti: e.tensor_tensor_scan(
                        out=tr[:, so:so + sl], data0=ta[:, so:so + sl], data1=ti[:, so:so + sl], initial=init,
                        op0=ALU.mult, op1=ALU.add), [kta, kti, ik], [ktr])
                    if is_s:
                        o2 = SL["hs"][0] + (l * 8 + c) * 4 + si
                        vcopy(so_sb[:, o2:o2 + 1], tr[:, so + sl - 1:so + sl], [ktr], ["so_sb"])
                    else:
                        vcopy(hP[:, l, c:c + 1], tr[:, so + sl - 1:so + sl], [ktr], ["hP"])
                tt(yB[:, c, 0:NT], tr[:, 0:NT], slg[:, 0:NT], ALU.mult, [ktr, kslg], ["yB"])
                yield
            gens = [chunk_gen(fc) for fc in range(2)]
            for _stage in range(7):
                for g_ in gens:
                    next(g_)
        P.fence([f"TB{i}" for i in range(10)] + ["B3x0", "B3x1", "slg0", "slg1"], ["F1", "F2", "B3", "B1"])

        chunks = tc["chunks"]
        nch = len(chunks)
        L = tc["L"]
        mid = L // 2 - 1
        rm0 = 0 if not is_s else NTP
        hblocks = tc["hblocks"]
        nhb = len(hblocks)
        for qt in range(4):
            G = F1[:, 0:1024].rearrange("p (h n) -> p h n", h=2)
            Qf = F1[:, 1024:2048].rearrange("p (h n) -> p h n", h=2)
            E = F1[:, 2048:3072].rearrange("p (h n) -> p h n", h=2)
            KtT = B1[:, 0:1024].rearrange("p (h n) -> p h n", h=2)
            KhT = B1[:, 1024:2048].rearrange("p (h n) -> p h n", h=2)
            QtT = B1[:, 2048:3072].rearrange("p (h n) -> p h n", h=2)
            Vt = B2[:, 0:nhb * 256].rearrange("p (b n) -> p b n", b=nhb)
            Ktok = B2[:, 2048:2048 + nhb * 256].rearrange("p (b h n) -> p b h n", b=nhb, h=2)
            shg = B3[:, 0:1024].rearrange("p (h n) -> p h n", h=2)
            AmA = B3[:, 1024:2048].rearrange("p (h n) -> p h n", h=2)
            Sb = B3[:, 2048:2048 + 1024].rearrange("p (i n) -> p i n", i=8)
            otk = F2[:, 0:nhb * 256].rearrange("p (b h n) -> p b h n", b=nhb, h=2)
            wv, wk = load_w(w_in_src(l, OFF["hf"] + qt * 256), 16, 256)
            for fc in range(2):
                hh = fc
                hd = qt * 2 + hh
                ps, pk = fm_group(wv, wk, fc * 128, 16, hT, hkeys)
                act(E[:, hh, 0:NT], ps[:, 0:NT], AF.Sigmoid, [pk], [f"E{hh}"])
                ts(G[:, hh, 0:NT], E[:, hh, 0:NT], oml[:, l, hd:hd + 1], lbs[:, l, hd:hd + 1], ALU.mult, ALU.add,
                   [f"E{hh}", "oml", "lbs"], [f"G{hh}"])
                ts(KtT[:, hh, 0:NT], G[:, hh, 0:NT], -1.0, 1.0, ALU.mult, ALU.add, [f"G{hh}"], [f"Kt{hh}"])
                act(E[:, hh, 0:NT], G[:, hh, 0:NT], AF.Ln, [f"G{hh}"], [f"E{hh}"])
                P.op("dve", lambda e, hh=hh, G=G, E=E, rm0=rm0, NT=NT: e.tensor_tensor_scan(
                    out=G[:, hh, 0:NT], data0=rmask_bf[:, rm0:rm0 + NT], data1=E[:, hh, 0:NT], initial=0.0,
                    op0=ALU.mult, op1=ALU.add), [f"E{hh}", "rmask"], [f"G{hh}"])
            wv, wk = load_w(w_in_src(l, OFF["hq"] + qt * 256), 16, 256)
            for fc in range(2):
                hh = fc
                ps, pk = fm_group(wv, wk, fc * 128, 16, hT, hkeys)
                act(Qf[:, hh, 0:NT], ps[:, 0:NT], AF.Silu, [pk], [f"Qf{hh}"])
            wv, wk = load_w(w_in_src(l, OFF["hi"] + qt * 256), 16, 256)
            for b, (off, ln) in enumerate(hblocks):
                ps, pk = tm_group(wv, wk, off, ln, 256)
                act(Vt[0:ln, b, :], ps[0:ln, 0:256], AF.Copy, [pk], ["Vt"])
            wv, wk = load_w(w_in_src(l, OFF["hg"] + qt * 256), 16, 256)
            for fc in range(2):
                hh = fc
                ps, pk = fm_group(wv, wk, fc * 128, 16, hT, hkeys)
                act(shg[:, hh, 0:NT], ps[:, 0:NT], AF.Silu, [pk], [f"shg{hh}"])
            def head_gen(hh):
                Gc = G[:, hh, 0:NT].rearrange("p (c n) -> p c n", n=L)
                Ev = E[:, hh, 0:NT].rearrange("p (c n) -> p c n", n=L)

                def rescan():
                    P.op("dve", lambda e, hh=hh, G=G, E=E, rm0=rm0, NT=NT: e.tensor_tensor_scan(
                        out=G[:, hh, 0:NT], data0=rmask_bf[:, rm0:rm0 + NT], data1=E[:, hh, 0:NT], initial=0.0,
                        op0=ALU.mult, op1=ALU.add), [f"E{hh}", "rmask"], [f"G{hh}"])
                vcopy(hgs[:, 1, hh, 0:nch], Gc[:, :, mid], [f"G{hh}"], [f"hgs{hh}"])
                vcopy(hgs[:, 4, hh, 0:nch], Gc[:, :, L - 1], [f"G{hh}"], [f"hgs{hh}"])
                act(hgs[:, 2, hh, 0:nch], Gc[:, :, mid], AF.Exp, [f"G{hh}"], [f"hgs{hh}"])
                act(hgs[:, 3, hh, 0:nch], Gc[:, :, L - 1], AF.Exp, [f"G{hh}"], [f"hgs{hh}"])
                tt(hgs[:, 0, hh, 0:nch], hgs[:, 4, hh, 0:nch], hgs[:, 1, hh, 0:nch], ALU.subtract, [f"hgs{hh}"], [f"hgs{hh}"])
                tt(Ev[:, :, 0], Ev[:, :, 0], hgs[:, 4, hh, 0:nch], ALU.subtract, [f"E{hh}", f"hgs{hh}"], [f"E{hh}"])
                rescan()
                yield
                act(G[:, hh, 0:NT], G[:, hh, 0:NT], AF.Exp, [f"G{hh}"], [f"G{hh}"], scale=-1.0)
                yield
                tt(KhT[:, hh, 0:NT], KtT[:, hh, 0:NT], G[:, hh, 0:NT], ALU.mult, [f"Kt{hh}", f"G{hh}"], [f"Kh{hh}"])
                tt(Ev[:, :, 0], Ev[:, :, 0], hgs[:, 0, hh, 0:nch], ALU.add, [f"E{hh}", f"hgs{hh}"], [f"E{hh}"])
                rescan()
                yield
                act(E[:, hh, 0:NT], G[:, hh, 0:NT], AF.Exp, [f"G{hh}"], [f"E{hh}"], scale=-1.0)
                act(G[:, hh, 0:NT], G[:, hh, 0:NT], AF.Exp, [f"G{hh}"], [f"G{hh}"])
                for b, (off, ln) in enumerate(hblocks):
                    ps2, pk2 = newps()
                    pb = ps2[:].bitcast(BF16)
                    transpose(pb[0:ln, 0:128], KhT[:, hh, off:off + ln], ident_bf[:], [f"Kh{hh}", "ident_bf"], [pk2])
                    vcopy(Ktok[0:ln, b, hh, :], pb[0:ln, 0:128], [pk2], [f"Ktok{hh}"])
                yield
                tt(KtT[:, hh, 0:NT], KtT[:, hh, 0:NT], E[:, hh, 0:NT], ALU.mult, [f"Kt{hh}", f"E{hh}"], [f"Kt{hh}"])
                tt(QtT[:, hh, 0:NT], Qf[:, hh, 0:NT], G[:, hh, 0:NT], ALU.mult, [f"Qf{hh}", f"G{hh}"], [f"Qt{hh}"])
                yield
            hgens = [head_gen(hh) for hh in range(2)]
            for _stage in range(5):
                for g_ in hgens:
                    next(g_)
            R = 32 if is_s else 64
            maskA = (maskS_u32 if is_s else maskP_u32)
            for hh in range(2):
                psA, pkA = newps()
                for ch in chunks:
                    ci, co, p0, ln = ch["ci"], ch["off"], ch["p0"], ch["ln"]
                    mm(psA[p0:p0 + ln, ci * 32:ci * 32 + ln], KtT[:, hh, co:co + ln], QtT[:, hh, co:co + ln], True, True,
                       [f"Kt{hh}", f"Qt{hh}"], [pkA] if ci == 0 else [], touch=[pkA])
                P.op("dve", lambda e, AmA=AmA, psA=psA, hh=hh, R=R, nch=nch, maskA=maskA: e.copy_predicated(
                    out=AmA[0:R, hh, 0:nch * 32], mask=maskA[0:R, 0:nch * 32], data=psA[0:R, 0:nch * 32]),
                    [pkA, "const"], [f"AmA{hh}"])
            nob = (nhb + 3) // 4
            pso_b = [[newps() for _ in range(nob)] for hh in range(2)]
            pss_b = [newps() for _ in range(2)]
            units = [(ch, hh) for ch in chunks for hh in range(2)]
            ngrp = (len(units) + 3) // 4
            pss_q = {}

            def emit_pss_group(g):
                bank, bk = pss_b[g % 2]
                for j in range(4):
                    ui = g * 4 + j
                    if ui >= len(units):
                        break
                    ch, hh = units[ui]
                    p0, ln, b = ch["p0"], ch["ln"], ch["blk"]
                    mm(bank[:, j * 128:(j + 1) * 128], Ktok[p0:p0 + ln, b, hh, :], Vt[p0:p0 + ln, b, hh * 128:(hh + 1) * 128], True, True,
                       [f"Ktok{hh}", "Vt"], [bk], touch=[bk])
                    pss_q[ui] = (bank[:, j * 128:(j + 1) * 128], bk)
            for g in range(min(2, ngrp)):
                emit_pss_group(g)
            sb_of = {}
            sbi = 0

            def emit_sb(ch, hh):
                nonlocal_sbi = sb_cnt[0]
                sb_cnt[0] += 1
                hd = qt * 2 + hh
                Sh = Sst[:, l, hd * 128:(hd + 1) * 128]
                buf = Sb[:, nonlocal_sbi % 8, :]
                key = f"Sb{nonlocal_sbi % 8}"
                ts(buf, Sh, hgs[:, 2, hh, ch["ci"]:ch["ci"] + 1], None, ALU.mult, None, [f"S{l}_{hd}", f"hgs{hh}"], [key])
                sb_of[(ch["ci"], hh)] = (buf, key)
            sb_cnt = [0]
            if not is_s:
                for hh in range(2):
                    emit_sb(chunks[0], hh)
            first_pso = {}
            for ui, (ch, hh) in enumerate(units):
                ci, co, p0, ln, b, si = ch["ci"], ch["off"], ch["p0"], ch["ln"], ch["blk"], ch["seg"]
                hd = qt * 2 + hh
                Sh = Sst[:, l, hd * 128:(hd + 1) * 128]
                skey = f"S{l}_{hd}"
                if is_s:
                    dma("sp", Sh, Ssin[l, si, :, hd * 128:(hd + 1) * 128], f"sld{hd}", [], [skey])
                    emit_sb(ch, hh)
                sbuf_, sbk = sb_of[(ci, hh)]
                pso, pko = pso_b[hh][b // 4]
                oc = (b % 4) * 128
                fk = (hh, b // 4)
                mm(pso[p0:p0 + ln, oc:oc + 128], QtT[:, hh, co:co + ln], sbuf_, True, False, [f"Qt{hh}", sbk],
                   [pko])
                first_pso[fk] = True
                mm(pso[p0:p0 + ln, oc:oc + 128], AmA[p0:p0 + ln, hh, ci * 32:ci * 32 + ln], Vt[p0:p0 + ln, b, hh * 128:(hh + 1) * 128],
                   False, True, [f"AmA{hh}", "Vt"], [], touch=[pko])
                pss, pssk = pss_q.pop(ui)
                stt(Sh, Sh, hgs[:, 3, hh, ci:ci + 1], pss, ALU.mult, ALU.add, [skey, f"hgs{hh}", pssk], [skey])
                if is_s:
                    dma("sp", Ss_out[l, si, :, hd * 128:(hd + 1) * 128], Sh, f"ostS{hd}", [skey], [])
                else:
                    if ci + 1 < nch:
                        emit_sb(chunks[ci + 1], hh)
                    elif last_tile:
                        dma("sp", Sp_out[l, :, hd * 128:(hd + 1) * 128], Sh, f"ostS{hd}", [skey], [])
                if ui % 4 == 3 and ui // 4 + 2 < ngrp:
                    emit_pss_group(ui // 4 + 2)
            for hh in range(2):
                for ob in range(nob):
                    pso, pko = pso_b[hh][ob]
                    nbl = min(4, nhb - ob * 4)
                    act(otk[0:R, ob * 4:ob * 4 + nbl, hh, :], pso[0:R, 0:nbl * 128].rearrange("p (b n) -> p b n", b=nbl), AF.Copy,
                        [pko], [f"otk{b_}" for b_ in range(ob * 4, ob * 4 + nbl)])
            onb = const_sb[:, CL["onormb"][0] + l * 128:CL["onormb"][0] + (l + 1) * 128]
            for b, (off, ln) in enumerate(hblocks):
                for hh in range(2):
                    act(B3[0:ln, 3584:3584 + 128], otk[0:ln, b, hh, :], AF.Square, [f"otk{b}"], ["junk", "ssq"],
                        accum_out=ssq[0:ln, hh:hh + 1])
                act(rsq[0:ln, 0:2], ssq[0:ln, 0:2], AF.Ln, ["ssq"], ["rsq"], bias=EPS, scale=1.0 / 128)
                act(rsq[0:ln, 0:2], rsq[0:ln, 0:2], AF.Exp, ["rsq"], ["rsq"], scale=-0.5)
                for hh in range(2):
                    hd = qt * 2 + hh
                    onk = f"on{hh % 2}"
                    on = B3[:, 3328 + (hh % 2) * 128:3328 + (hh % 2) * 128 + 128]
                    stt(on[0:ln, :], otk[0:ln, b, hh, :], rsq[0:ln, hh:hh + 1], onb[0:ln, :], ALU.mult, ALU.mult,
                        [f"otk{b}", "rsq", "const"], [onk])
                    ps2, pk2 = newps()
                    pb = ps2[:].bitcast(BF16)
                    transpose(pb[:, 0:ln], on[0:ln, :], ident_bf[0:ln, 0:ln], [onk, "ident_bf"], [pk2])
                    tt(yC[:, hd, off:off + ln], pb[:, 0:ln], shg[:, hh, off:off + ln], ALU.mult, [pk2, f"shg{hh}"], ["yC"])
            allk = [f"G{h}" for h in range(4)] + [f"E{h}" for h in range(4)] + [f"Kt{h}" for h in range(4)] + \
                   [f"Qt{h}" for h in range(4)] + [f"Ktok{h}" for h in range(4)] + [f"shg{h}" for h in range(4)] + \
                   [f"Kh{h}" for h in range(2)] + [f"Qf{h}" for h in range(2)] + [f"AmA{h}" for h in range(2)] + \
                   ["Vt", "junk", "on0", "on1", "tmpS0", "tmpS1"] + [f"Sb{i}" for i in range(8)] + [f"Am{i}" for i in range(4)] + \
                   [f"otk{b}" for b in range(8)]
            P.fence(allk, allk + ["F1", "F2", "B1", "B2", "B3"])
        allk = [f"G{h}" for h in range(4)] + [f"E{h}" for h in range(4)] + [f"Kt{h}" for h in range(4)] + \
               [f"Qt{h}" for h in range(4)] + [f"Ktok{h}" for h in range(4)] + [f"shg{h}" for h in range(4)] + \
               [f"Kh{h}" for h in range(2)] + [f"Qf{h}" for h in range(2)] + [f"AmA{h}" for h in range(2)] + \
               ["Vt", "junk", "on0", "on1", "tmpS0", "tmpS1"] + [f"Sb{i}" for i in range(8)] + [f"Am{i}" for i in range(4)] + \
               [f"otk{b}" for b in range(8)]
        if debug and l == 0 and (is_s or tile_index[0] == 0):
            di = 1 if is_s else 0
            for bi, (yy, yk) in enumerate(((yA, "yA"), (yB, "yB"), (yC, "yC"))):
                dma("pool", dbg[di, bi], yy[:].rearrange("p c n -> p (c n)"), f"dbg{bi}", [yk], [])
        mkeys = [f"mT{j}" for j in range(16)]
        tkeys = [f"TM{i}" for i in range(4)] + [f"ACC{i}" for i in range(4)]
        P.fence(allk + ["F1", "F2", "B1", "B2", "B3"], mkeys + tkeys)

        mT = F1[:].bitcast(BF16)[:, 0:16 * NTP].rearrange("p (k n) -> p k n", k=16)
        ACC = [F2[:, i * 512:(i + 1) * 512] for i in range(4)]
        B2f = B2[:].bitcast(F32)
        TM = [B2f[:, i * 512:(i + 1) * 512] for i in range(4)]
        ys = [(yA, "yA"), (yB, "yB"), (yC, "yC")]
        tmi = 0
        for j4 in range(4):
            for br in range(3):
                wbv, wbk = load_w(w_br[l, br * 1024:(br + 1) * 1024, j4 * 512:(j4 + 1) * 512].rearrange("(k p) n -> p k n", p=128), 8, 512)
                pps = [fm_group(wbv, wbk, jj_ * 128, 8, ys[br][0], [ys[br][1]]) for jj_ in range(4)]
                for h2 in range(2):
                    wgv, wgk = load_w(w_in_src(l, OFF["mg"] + br * 2048 + j4 * 512 + h2 * 256), 16, 256)
                    for fc in range(2):
                        jj = h2 * 2 + fc
                        j = j4 * 4 + jj
                        pg, pgk = fm_group(wgv, wgk, fc * 128, 16, hT, hkeys)
                        pp, ppk = pps[jj]
                        t = TM[tmi % 4]
                        tk = f"TM{tmi % 4}"
                        tmi += 1
                        act(t[:, 0:NT], pg[:, 0:NT], AF.Sigmoid, [pgk], [tk])
                        if br == 0:
                            tt(ACC[jj][:, 0:NT], t[:, 0:NT], pp[:, 0:NT], ALU.mult, [tk, ppk], [f"ACC{jj}"])
                        else:
                            tt(t[:, 0:NT], t[:, 0:NT], pp[:, 0:NT], ALU.mult, [tk, ppk], [tk])
                            if br == 1:
                                tt(ACC[jj][:, 0:NT], ACC[jj][:, 0:NT], t[:, 0:NT], ALU.add, [tk, f"ACC{jj}"], [f"ACC{jj}"])
                            else:
                                tt(mT[:, j, 0:NT], ACC[jj][:, 0:NT], t[:, 0:NT], ALU.add, [tk, f"ACC{jj}"], [f"mT{j}"])
        if debug and l == 0 and (is_s or tile_index[0] == 0):
            di = 1 if is_s else 0
            dma("pool", dbg[di, 3], mT[:, 0:8, :].rearrange("p c n -> p (c n)"), "dbg3", mkeys, [])
        for j2 in range(8):
            wv, wk = load_w(w_out[l, :, j2 * 256:(j2 + 1) * 256].rearrange("(k p) n -> p k n", p=128), 16, 256)
            for fc in range(2):
                j = j2 * 2 + fc
                ps, pk = fm_group(wv, wk, fc * 128, 16, mT, mkeys)
                for (so, sl, sq) in segs:
                    stt(xT[:, j, so:so + sl], ps[:, so:so + sl], modT[:, l, 32 + j, sq:sq + 1], xT[:, j, so:so + sl],
                        ALU.mult, ALU.add, [pk, "modT", f"xT{j}"], [f"xT{j}"])
        P.fence(mkeys + tkeys, ["F1", "F2", "B1", "B2", "B3"])

    def gelu_compose(out, ps, pk, okey, npart, n):
        raise NotImplementedError

    tiles = [("p", i) for i in range(n_pt)] + [("s", 0)]
    tile_index = [0]
    for (kind, ti) in tiles:
        tc = tile_cfg(kind)
        NT = tc["NT"]
        if kind == "p":
            src = xpT[:, ti * NTP:(ti + 1) * NTP]
            dst = yT[:, ti * NTP:(ti + 1) * NTP]
        else:
            src = xsT[:, :]
            dst = yT[:, SEQL:SEQL + NSMP]
        xkeys = [f"xT{kc}" for kc in range(KC)]
        dma("sp", xT[:, :, 0:NT], src.rearrange("(k p) n -> p k n", p=128), "xld", [], xkeys)
        tile_index[0] = ti
        for l in range(depth):
            layer(tc, l, last_tile=(kind == "p" and ti == n_pt - 1))
        rms_stats(NT)
        for kc in range(KC):
            stt(xT[:, kc, 0:NT], xT[:, kc, 0:NT], vcol("finalg", kc), rstd[:, 0:NT], ALU.mult, ALU.mult,
                [f"xT{kc}", "vec", "rstd"], [f"xT{kc}"])
        dma("sp", dst.rearrange("(k p) n -> p k n", p=128), xT[:, :, 0:NT], "osty", xkeys, [])
        if kind == "p" and ti == n_pt - 1:
            o = SL["convp"][0]
            vcopy(so_sb[:, o:o + depth * 24], convP[:].rearrange("p l c t -> p (l c t)"), ["convP"], ["so_sb"])
            o = SL["hp"][0]
            vcopy(so_sb[:, o:o + depth * 8], hP[:].rearrange("p l c -> p (l c)"), ["hP"], ["so_sb"])
    dma("sp", small_out[:, :], so_sb[:], "osts", ["so_sb"], [])
    P.final_wait("sp", [x for x in P.dmasems if x.startswith("ost") or x.startswith("dbg")])

    sem_names = list(ENGS) + P.dmasems
    sems = {n: es.enter_context(nc.semaphore(n)) for n in sem_names}
    block = es.enter_context(nc.Block())

    def run(engh, items):
        for it in items:
            if it[0] == "w":
                engh.wait_ge(sems[it[1]], it[2])
            else:
                inst = it[1](engh)
                if it[2] is not None:
                    inst.then_inc(sems[it[2]], it[3])

    @block.tensor
    def _(e):
        run(e, P.lists["pe"])

    @block.scalar
    def _(e):
        run(e, P.lists["act"])

    @block.vector
    def _(e):
        run(e, P.lists["dve"])

    @block.gpsimd
    def _(e):
        run(e, P.lists["pool"])

    @block.sync
    def _(e):
        run(e, P.lists["sp"])

    es.close()
    return nc


def pmajor(a):
    a = np.asarray(a, np.float32)
    lead = a.shape[:-1]
    n = a.shape[-1] // 128
    a = a.reshape(*lead, n, 128)
    a = np.moveaxis(a, -1, 0)
    return np.ascontiguousarray(a)


def prep_core(inp, core, depth, seql):
    VL, NV = vec_layout(depth)
    CL, NCONST = const_layout(depth)
    ps = core % 4
    ss = slice(4 * core, 4 * core + 4)
    vec = np.zeros((128, NV), np.float32)

    def put(name, arr):
        o, n = VL[name]
        vec[:, o:o + n] = arr.reshape(128, n)

    cc = np.concatenate([inp["c_prompt"][ps:ps + 1], inp["c_sample"][ss]], axis=0)
    put("c", np.moveaxis(pmajor(cc), 1, 2))
    put("bada", pmajor(inp["b_ada"][:depth]))
    put("normg", pmajor(inp["norm_g"][:depth]))
    put("finalg", pmajor(inp["final_g"]))
    put("ba", pmajor(inp["lru_ba"][:depth]))
    put("bx", pmajor(inp["lru_bx"][:depth]))
    put("lam", pmajor(inp["lru_lambda"][:depth]))
    put("convb", pmajor(inp["lru_conv_b"][:depth]))
    put("convw", pmajor(inp["lru_conv_w"][:depth]))
    put("hglb", pmajor(inp["hg_lb"][:depth]))
    cs = inp["state_rglru_conv"][:depth, ss]
    cs = pmajor(cs)
    put("convs", np.transpose(cs, (0, 1, 4, 2, 3)))
    hs = pmajor(inp["state_rglru_h"][:depth, ss])
    put("hs", np.transpose(hs, (0, 1, 3, 2)))

    m = {}
    m["xpT"] = np.ascontiguousarray(inp["x_prompt"][ps, :seql].T)
    m["xsT"] = np.ascontiguousarray(inp["x_sample"][ss].reshape(NSMP, D).T)
    m["vecs"] = vec
    S = inp["state_hgrn2"][:depth, ss]
    m["Ssin"] = np.ascontiguousarray(np.transpose(S, (0, 1, 3, 2, 4)).reshape(depth, 4, 128, 1024))
    return m


def prep_shared(inp, depth):
    CL, NCONST = const_layout(depth)
    con = np.zeros((128, NCONST), np.float32)

    def putc(name, arr):
        o, n = CL[name]
        con[:, o:o + n] = arr
    putc("ident", np.eye(128, dtype=np.float32))
    putc("triu", np.triu(np.ones((128, 128), np.float32)))
    rp = np.ones(512, np.float32)
    rp[::32] = 0
    putc("rmask_p", np.broadcast_to(rp, (128, 512)))
    rs = np.ones(128, np.float32)
    rs[::32] = 0
    putc("rmask_s", np.broadcast_to(rs, (128, 128)))
    putc("onormb", np.broadcast_to(inp["hg_onorm_g"][:depth].reshape(1, depth * 128), (128, depth * 128)))
    putc("ones", np.ones((128, 128), np.float32))
    t32 = np.triu(np.ones((32, 32), np.float32))
    mp = np.zeros((128, 512), np.float32)
    for ci in range(16):
        p0 = (ci % 2) * 32
        mp[p0:p0 + 32, ci * 32:(ci + 1) * 32] = t32
    putc("maskP", mp)
    ms = np.zeros((128, 128), np.float32)
    for ci in range(4):
        ms[0:32, ci * 32:(ci + 1) * 32] = t32
    putc("maskS", ms)
    m = {"consts": con}
    m["w_ada"] = np.ascontiguousarray(inp["w_ada"][:depth])
    m["w_in"] = np.ascontiguousarray(inp["w_in"][:depth])
    m["w_branch"] = np.ascontiguousarray(inp["w_branch"][:depth])
    m["w_out"] = np.ascontiguousarray(inp["w_out"][:depth])
    m["gmw"] = np.ascontiguousarray(np.transpose(inp["gm_ws"][:depth], (0, 3, 1, 2)).reshape(depth, 128, 1024))
    m["gmb"] = np.ascontiguousarray(inp["gm_bs"][:depth].reshape(depth, 1, 1024))
    m["vnb"] = np.ascontiguousarray(np.broadcast_to(inp["gm_vnorm_g"][:depth, None, :], (depth, 128, 1024)))
    m["lwa"] = np.ascontiguousarray(np.transpose(inp["lru_wa"][:depth], (0, 2, 1, 3)).reshape(depth, 128, 1024))
    m["lwx"] = np.ascontiguousarray(np.transpose(inp["lru_wx"][:depth], (0, 2, 1, 3)).reshape(depth, 128, 1024))
    return m


def unpmajor(a):
    a = np.moveaxis(a, 0, -1)
    return np.ascontiguousarray(a).reshape(*a.shape[:-2], a.shape[-2] * 128)


def assemble(results, depth, seql, ncores):
    SL, NSO = so_layout(depth)
    nseq_p = min(4, ncores)
    y_prompt = np.zeros((nseq_p, seql, D), np.float32)
    y_sample = np.zeros((4 * ncores, 32, D), np.float32)
    conv_p = np.zeros((depth, nseq_p, 3, 1024), np.float32)
    h_p = np.zeros((depth, nseq_p, 1024), np.float32)
    S_p = np.zeros((depth, nseq_p, 8, 128, 128), np.float32)
    conv_s = np.zeros((depth, 4 * ncores, 3, 1024), np.float32)
    h_s = np.zeros((depth, 4 * ncores, 1024), np.float32)
    S_s = np.zeros((depth, 4 * ncores, 8, 128, 128), np.float32)
    v_s = np.zeros((depth, 4 * ncores, 32, 1024), np.float32)
    for c, r in enumerate(results):
        yT = r["yT"]
        so = r["small_out"]

        def get(name, shape):
            o, n = SL[name]
            return so[:, o:o + n].reshape(128, *shape)
        if c < 4:
            y_prompt[c] = yT[:, :seql].T
            cp = get("convp", (depth, 8, 3))
            conv_p[:, c] = np.transpose(unpmajor(np.transpose(cp, (0, 1, 3, 2))), (0, 1, 2))
            h_p[:, c] = unpmajor(get("hp", (depth, 8)))
            S_p[:, c] = np.transpose(r["Sp_out"].reshape(depth, 128, 8, 128), (0, 2, 1, 3))
        ss = slice(4 * c, 4 * c + 4)
        y_sample[ss] = yT[:, seql:].T.reshape(4, 32, D)
        cs = get("convs", (depth, 8, 4, 3))
        conv_s[:, ss] = unpmajor(np.transpose(cs, (0, 1, 3, 4, 2)))
        hs = get("hs", (depth, 8, 4))
        h_s[:, ss] = unpmajor(np.transpose(hs, (0, 1, 3, 2)))
        S_s[:, ss] = np.transpose(r["Ss_out"].reshape(depth, 4, 128, 8, 128), (0, 1, 3, 2, 4))
        v_s[:, ss] = r["vs_out"]
    return (y_prompt, y_sample, conv_p, h_p, S_p, conv_s, h_s, S_s, v_s)


_NC_CACHE = {}


def run(inp, depth=DEPTH, n_pt=SEQ // NTP, ncores=8, trace=False):
    key = (depth, n_pt)
    if key not in _NC_CACHE:
        _NC_CACHE[key] = build(depth, n_pt)
    nc = _NC_CACHE[key]
    seql = n_pt * NTP
    shared = prep_shared(inp, depth)
    in_maps = []
    for c in range(ncores):
        m = dict(shared)
        m.update(prep_core(inp, c, depth, seql))
        in_maps.append(m)
    res = run_bass_kernel_spmd(nc, in_maps, core_ids=list(range(ncores)), **({"trace": True} if trace else {}))
    return assemble(res.results, depth, seql, ncores), res


def kernel(**inputs):
    inp = {k: np.asarray(v) for k, v in inputs.items()}
    outs, _ = run(inp)
    return outs
```

```python
import numpy as np
from contextlib import ExitStack
import concourse.bass as bass
import concourse.mybir as mybir
from concourse.bass_utils import run_bass_kernel_spmd

F32 = mybir.dt.float32
BF16 = mybir.dt.bfloat16
AF = mybir.ActivationFunctionType
ALU = mybir.AluOpType

D = 2048
KC = 16
DEPTH = 4
SEQ = 4096
NTP = 512
NSMP = 128
IN_W = 15360
EPS = 1e-6
OFF = dict(au=0, av=1024, ag=2048, lx=3072, lg=4096, hq=5120, hf=6144, hi=7168, hg=8192, mg=9216)
ENGS = ("pe", "act", "dve", "pool", "sp")
USE_GELU_TANH = True


def vec_layout(depth):
    lay = {}
    o = 0
    for name, n in (("c", 16 * 5), ("bada", depth * 48), ("normg", depth * 16), ("finalg", 16),
                    ("ba", depth * 8), ("bx", depth * 8), ("lam", depth * 8), ("convb", depth * 8),
                    ("convw", depth * 4 * 8), ("hglb", depth * 8),
                    ("convs", depth * 8 * 4 * 3), ("hs", depth * 8 * 4)):
        lay[name] = (o, n)
        o += n
    return lay, o


def const_layout(depth):
    lay = {}
    o = 0
    for name, n in (("ident", 128), ("triu", 128), ("onormb", depth * 128), ("ones", 128),
                    ("maskP", 512), ("maskS", 128), ("rmask_p", 512), ("rmask_s", 128)):
        lay[name] = (o, n)
        o += n
    return lay, o


SO_LAYOUT = None


def so_layout(depth):
    lay = {}
    o = 0
    for name, n in (("convp", depth * 8 * 3), ("hp", depth * 8), ("convs", depth * 8 * 4 * 3), ("hs", depth * 8 * 4)):
        lay[name] = (o, n)
        o += n
    return lay, o


class Prog:
    def __init__(self):
        self.lists = {e: [] for e in ENGS}
        self.cnt = {e: 0 for e in ENGS}
        self.known = {e: {} for e in ENGS}
        self.lastw = {}
        self.readers = {}
        self.dmacnt = {}
        self.dmasems = []

    def _deps(self, eng, reads, writes):
        need = {}

        def add(t):
            if t is None:
                return
            s, v = t
            if need.get(s, 0) < v:
                need[s] = v
        for k in reads:
            add(self.lastw.get(k))
        for k in writes:
            add(self.lastw.get(k))
            for s, v in self.readers.get(k, {}).items():
                add((s, v))
        kn = self.known[eng]
        for s, v in need.items():
            if kn.get(s, 0) < v:
                self.lists[eng].append(("w", s, v))
                kn[s] = v

    def _commit(self, tok, reads, writes):
        s, v = tok
        for k in writes:
            self.lastw[k] = tok
            self.readers[k] = {}
        for k in reads:
            r = self.readers.setdefault(k, {})
            if r.get(s, 0) < v:
                r[s] = v

    def op(self, eng, fn, reads=(), writes=(), inc=True):
        self._deps(eng, reads, writes)
        if inc:
            self.cnt[eng] += 1
            tok = (eng, self.cnt[eng])
            self.lists[eng].append(("i", fn, eng, 1))
        else:
            tok = (eng, self.cnt[eng] + 1)
            self.lists[eng].append(("i", fn, None, 0))
        self._commit(tok, reads, writes)

    def dma(self, q, fn, sem, reads=(), writes=()):
        self._deps(q, reads, writes)
        if sem not in self.dmacnt:
            self.dmacnt[sem] = 0
            self.dmasems.append(sem)
        self.dmacnt[sem] += 16
        tok = (sem, self.dmacnt[sem])
        self.lists[q].append(("i", fn, sem, 16))
        self._commit(tok, reads, writes)

    def fence(self, oldkeys, newkeys):
        acc = {}
        for k in oldkeys:
            t = self.lastw.get(k)
            if t is not None and acc.get(t[0], 0) < t[1]:
                acc[t[0]] = t[1]
            for s, v in self.readers.get(k, {}).items():
                if acc.get(s, 0) < v:
                    acc[s] = v
        for k in newkeys:
            r = self.readers.setdefault(k, {})
            for s, v in acc.items():
                if r.get(s, 0) < v:
                    r[s] = v

    def final_wait(self, eng, sems):
        for s in sems:
            v = self.dmacnt.get(s, 0)
            if v:
                self.lists[eng].append(("w", s, v))


def tile_cfg(kind):
    if kind == "p":
        NT = NTP
        blocks = [(i * 128, 128) for i in range(4)]
        segs = [(0, NT, 0)]
        L = 32
        chunks = []
        for i in range(NT // L):
            chunks.append(dict(blk=i // 2, p0=(i % 2) * 32, ln=L, off=i * L, seg=0, ci=i))
        hblocks = [(i * 64, 64) for i in range(8)]
        return dict(kind="p", NT=NT, blocks=blocks, hblocks=hblocks, segs=segs, L=L, chunks=chunks, rmask="rmask_p")
    NT = NSMP
    blocks = [(i * 32, 32) for i in range(4)]
    segs = [(i * 32, 32, 1 + i) for i in range(4)]
    chunks = [dict(blk=i, p0=0, ln=32, off=i * 32, seg=i, ci=i) for i in range(4)]
    return dict(kind="s", NT=NT, blocks=blocks, hblocks=blocks, segs=segs, L=32, chunks=chunks, rmask="rmask_s")


def build(depth=DEPTH, n_pt=SEQ // NTP, debug=False):
    SEQL = n_pt * NTP
    VL, NV = vec_layout(depth)
    CL, NCONST = const_layout(depth)
    SL, NSO = so_layout(depth)
    nc = bass.Bass("TRN2", target_bir_lowering=False)

    def din(name, shape):
        return nc.dram_tensor(name, shape, F32, kind="ExternalInput").ap()

    def dout(name, shape):
        return nc.dram_tensor(name, shape, F32, kind="ExternalOutput").ap()

    xpT = din("xpT", [D, SEQL])
    xsT = din("xsT", [D, NSMP])
    vecs = din("vecs", [128, NV])
    consts = din("consts", [128, NCONST])
    Ssin = din("Ssin", [depth, 4, 128, 1024])
    w_ada = din("w_ada", [depth, D, 3 * D])
    w_in = din("w_in", [depth, D, IN_W])
    w_br = din("w_branch", [depth, 3072, D])
    w_out = din("w_out", [depth, D, D])
    gmw = din("gmw", [depth, 128, 1024])
    gmb = din("gmb", [depth, 1, 1024])
    vnb = din("vnb", [depth, 128, 1024])
    lwa = din("lwa", [depth, 128, 1024])
    lwx = din("lwx", [depth, 128, 1024])

    yT = dout("yT", [D, SEQL + NSMP])
    small_out = dout("small_out", [128, NSO])
    Sp_out = dout("Sp_out", [depth, 128, 1024])
    Ss_out = dout("Ss_out", [depth, 4, 128, 1024])
    vs_out = dout("vs_out", [depth, 4, 32, 1024])
    dbg = dout("dbg", [2, 4, 128, 8 * NTP]) if debug else None

    P = Prog()
    es = ExitStack()

    def sb(name, shape, dt=F32):
        return es.enter_context(nc.sbuf_tensor(name, shape, dt))

    xT = sb("xT", [128, KC, NTP])
    hT = sb("hT", [128, KC, NTP], BF16)
    yA = sb("yA", [128, 8, NTP], BF16)
    yB = sb("yB", [128, 8, NTP], BF16)
    yC = sb("yC", [128, 8, NTP], BF16)
    NRING = 4
    ring = [sb(f"ring{i}", [128, 4096], BF16) for i in range(NRING)]
    F1 = sb("F1", [128, 4224])
    F2 = sb("F2", [128, 2048])
    B1 = sb("B1", [128, 4096], BF16)
    B2 = sb("B2", [128, 4096], BF16)
    B3 = sb("B3", [128, 4096], BF16)
    Sst = sb("Sst", [128, depth, 1024])
    rstd = sb("rstd", [128, NTP])
    vec_sb = sb("vec_sb", [128, NV])
    NCS = CL["rmask_p"][0]
    const_sb = sb("const_sb", [128, NCS])
    ident_bf = sb("ident_bf", [128, 128], BF16)
    ones_bf = sb("ones_bf", [128, 128], BF16)
    rmask_bf = sb("rmask_bf", [128, NTP + NSMP], BF16)
    scT = sb("scT", [128, 16, 5], BF16)
    modT = sb("modT", [128, depth, 48, 5])
    Amod = sb("Amod", [128, depth, 16, 5])
    lbs = sb("lbs", [128, depth, 8])
    oml = sb("oml", [128, depth, 8])
    s1 = sb("s1", [128, depth, 8])
    s2 = sb("s2", [128, depth, 8])
    sm = sb("sm", [128, 336])
    gmw_bf = sb("gmw_bf", [128, 1024], BF16)
    gmb_sb = sb("gmb_sb", [1, 1024])
    vnb_sb = sb("vnb_sb", [128, 1024])
    wa_bf = sb("wa_bf", [128, 1024], BF16)
    wx_bf = sb("wx_bf", [128, 1024], BF16)
    convP = sb("convP", [128, depth, 8, 3])
    hP = sb("hP", [128, depth, 8])
    so_sb = sb("so_sb", [128, NSO])
    hgs = sb("hgs", [128, 5, 2, 16])
    ssq = sb("ssq", [128, 16])
    rsq = sb("rsq", [128, 16])

    NPS = 8
    psb = [es.enter_context(nc.psum_tensor(f"ps{i}", [128, 512], F32)) for i in range(NPS)]
    ps_i = [0]

    def newps():
        i = ps_i[0]
        ps_i[0] = (i + 1) % NPS
        return psb[i], f"ps{i}"

    def vcol(name, idx=0, n=1):
        o, _ = VL[name]
        return vec_sb[:, o + idx:o + idx + n]

    def ccol(name, idx=0, n=1):
        o, _ = CL[name]
        return const_sb[:, o + idx:o + idx + n]

    def act(out, in_, func, reads, writes, bias=None, scale=None, accum_out=None):
        kw = {}
        if bias is not None:
            kw["bias"] = bias
        if scale is not None:
            kw["scale"] = scale
        if accum_out is not None:
            kw["accum_out"] = accum_out
        P.op("act", lambda e: e.activation(out=out, in_=in_, func=func, **kw), reads, writes)

    def tt(out, in0, in1, op, reads, writes, eng="dve"):
        P.op(eng, lambda e: e.tensor_tensor(out=out, in0=in0, in1=in1, op=op), reads, writes)

    def ts(out, in0, s1_, s2_, op0, op1, reads, writes, eng="dve"):
        if op1 is None:
            P.op(eng, lambda e: e.tensor_scalar(out=out, in0=in0, scalar1=s1_, scalar2=None, op0=op0), reads, writes)
        else:
            P.op(eng, lambda e: e.tensor_scalar(out=out, in0=in0, scalar1=s1_, scalar2=s2_, op0=op0, op1=op1), reads, writes)

    def stt(out, in0, scalar, in1, op0, op1, reads, writes):
        P.op("dve", lambda e: e.scalar_tensor_tensor(out=out, in0=in0, scalar=scalar, in1=in1, op0=op0, op1=op1), reads, writes)

    def vcopy(out, in_, reads, writes, eng="dve"):
        P.op(eng, lambda e: e.tensor_copy(out=out, in_=in_), reads, writes)

    def mm(out, lhsT, rhs, start, stop, reads, writes, touch=()):
        P.op("pe", lambda e: e.matmul(out, lhsT, rhs, start=start, stop=stop), reads, writes, inc=stop)
        for k in touch:
            P.lastw[k] = ("pe", P.cnt["pe"])

    def transpose(out, in_, ident, reads, writes):
        P.op("pe", lambda e: e.transpose(out, in_, ident), reads, writes)

    def dma(q, out, in_, sem, reads, writes):
        P.dma(q, lambda e: e.dma_start(out=out, in_=in_), sem, reads, writes)

    ring_i = [0]

    def load_w(src, kcn, ncn):
        i = ring_i[0]
        ring_i[0] = (i + 1) % NRING
        view = ring[i][:, 0:kcn * ncn].rearrange("p (k n) -> p k n", k=kcn)
        dma("pool", view, src, f"wsem{i}", [], [f"ring{i}"])
        return view, f"ring{i}"

    def w_in_src(l, c0, ncn=256):
        return w_in[l, :, c0:c0 + ncn].rearrange("(k p) n -> p k n", p=128)

    dma("sp", vec_sb[:], vecs[:, :], "c0", [], ["vec"])
    dma("sp", const_sb[:], consts[:, 0:NCS], "c1", [], ["const"])
    vcopy(ident_bf[:], ccol("ident", 0, 128), ["const"], ["ident_bf"])
    vcopy(ones_bf[:], ccol("ones", 0, 128), ["const"], ["ones_bf"])
    dma("pool", rmask_bf[:, 0:NTP + NSMP], consts[:, CL["rmask_p"][0]:CL["rmask_p"][0] + NTP + NSMP], "c2", [], ["rmask"])
    act(scT[:].rearrange("p k s -> p (k s)"), vcol("c", 0, 80), AF.Silu, ["vec"], ["scT"])
    triu_u32 = const_sb[:, CL["triu"][0]:CL["triu"][0] + 128].bitcast(mybir.dt.uint32)
    maskP_u32 = const_sb[:, CL["maskP"][0]:CL["maskP"][0] + 512].bitcast(mybir.dt.uint32)
    maskS_u32 = const_sb[:, CL["maskS"][0]:CL["maskS"][0] + 128].bitcast(mybir.dt.uint32)
    P.op("dve", lambda e: e.memset(B3[:, 1024:2048], 0.0), [], ["AmA0", "AmA1"])
    P.op("dve", lambda e: e.memset(Sst[:].rearrange("p l n -> p (l n)"), 0.0), [], [f"S{l}_{hd}" for l in range(depth) for hd in range(8)])
    P.op("dve", lambda e: e.memset(convP[:].rearrange("p l c t -> p (l c t)"), 0.0), [], ["convP"])
    P.op("dve", lambda e: e.memset(hP[:].rearrange("p l c -> p (l c)"), 0.0), [], ["hP"])
    ex = sm[:, 0:depth * 8].rearrange("p (l c) -> p l c", l=depth)
    act(sm[:, 0:depth * 8], vcol("hglb", 0, depth * 8), AF.Exp, ["vec"], ["sm"])
    ssum = sm[:, 64:72]
    vcopy(ssum, ex[:, 0, :], ["sm"], ["sm"])
    for l in range(1, depth):
        tt(ssum, ssum, ex[:, l, :], ALU.add, ["sm"], ["sm"])
    P.op("dve", lambda e: e.reciprocal(out=sm[:, 72:80], in_=ssum), ["sm"], ["sm"])
    P.op("dve", lambda e: e.memset(lbs[:, 0, :], 0.0), [], ["lbs"])
    for l in range(1, depth):
        tt(sm[:, 80:88], ex[:, l, :], sm[:, 72:80], ALU.mult, ["sm"], ["sm"])
        tt(lbs[:, l, :], lbs[:, l - 1, :], sm[:, 80:88], ALU.add, ["sm", "lbs"], ["lbs"])
    lbf = lbs[:].rearrange("p l c -> p (l c)")
    ts(oml[:].rearrange("p l c -> p (l c)"), lbf, -1.0, 1.0, ALU.mult, ALU.add, ["lbs"], ["oml"])
    act(sm[:, 128:128 + depth * 8], vcol("lam", 0, depth * 8), AF.Exp, ["vec"], ["sm"], scale=-1.0)
    act(sm[:, 192:192 + depth * 8], sm[:, 128:128 + depth * 8], AF.Ln, ["sm"], ["sm"], bias=1.0)
    ts(s1[:].rearrange("p l c -> p (l c)"), sm[:, 192:192 + depth * 8], -8.0, None, ALU.mult, None, ["sm"], ["s1"])
    ts(s2[:].rearrange("p l c -> p (l c)"), sm[:, 192:192 + depth * 8], -16.0, None, ALU.mult, None, ["sm"], ["s2"])

    for l in range(depth):
        for blk in range(24):
            wv, wk = load_w(w_ada[l, :, blk * 256:(blk + 1) * 256].rearrange("(k p) n -> p k n", p=128), 16, 256)
            for fc in range(2):
                ps, pk = newps()
                ch = blk * 2 + fc
                for kc in range(KC):
                    mm(ps[:, 0:5], wv[:, kc, fc * 128:(fc + 1) * 128], scT[:, kc, :], kc == 0, kc == KC - 1,
                       [wk, "scT"], [pk] if kc == 0 else [])
                ts(modT[:, l, ch, :], ps[:, 0:5], vcol("bada", l * 48 + ch), None, ALU.add, None, [pk, "vec"], ["modT"])
        ts(sm[:, 256:256 + 80], modT[:, l, 16:32, :].rearrange("p k s -> p (k s)"), 1.0, None, ALU.add, None, ["modT"], ["sm"])
        smv = sm[:, 256:256 + 80].rearrange("p (k s) -> p k s", k=16)
        for s in range(5):
            tt(Amod[:, l, :, s], smv[:, :, s], vcol("normg", l * 16, 16), ALU.mult, ["sm", "vec"], ["Amod"])

    def rms_stats(NT):
        for kc in range(KC):
            act(hT[:, kc, 0:NT], xT[:, kc, 0:NT], AF.Square, [f"xT{kc}"], [f"hT{kc}"])
        ps, pk = newps()
        for kc in range(KC):
            mm(ps[:, 0:NT], ones_bf[:], hT[:, kc, 0:NT], kc == 0, kc == KC - 1, ["ones_bf", f"hT{kc}"], [pk] if kc == 0 else [])
        act(rstd[:, 0:NT], ps[:, 0:NT], AF.Ln, [pk], ["rstd"], bias=EPS, scale=1.0 / D)
        act(rstd[:, 0:NT], rstd[:, 0:NT], AF.Exp, ["rstd"], ["rstd"], scale=-0.5)

    def layer(tc, l, last_tile):
        NT = tc["NT"]
        blocks = tc["blocks"]
        segs = tc["segs"]
        nb = len(blocks)
        is_s = tc["kind"] == "s"
        dma("pool", gmw_bf[:], gmw[l], "lc0", [], ["gmw_bf"])
        dma("pool", wa_bf[:], lwa[l], "lc1", [], ["wa_bf"])
        dma("pool", wx_bf[:], lwx[l], "lc2", [], ["wx_bf"])
        dma("sp", gmb_sb[:], gmb[l], "lc3", [], ["gmb_sb"])
        dma("sp", vnb_sb[:], vnb[l], "lc4", [], ["vnb_sb"])
        for g in range(8):
            tt(gmw_bf[:, g * 128:(g + 1) * 128], gmw_bf[:, g * 128:(g + 1) * 128], ccol("triu", 0, 128), ALU.mult,
               ["gmw_bf", "const"], ["gmw_bf"])
        rms_stats(NT)
        Tn = [F1[:, i * 512:(i + 1) * 512] for i in range(2)]
        P.fence(["F1"], ["Tn0", "Tn1"])
        for kc in range(KC):
            for (so, sl, sq) in segs:
                t = Tn[kc % 2]
                stt(t[:, so:so + sl], xT[:, kc, so:so + sl], Amod[:, l, kc, sq:sq + 1], rstd[:, so:so + sl], ALU.mult, ALU.mult,
                    [f"xT{kc}", "Amod", "rstd"], [f"Tn{kc % 2}"])
                act(hT[:, kc, so:so + sl], t[:, so:so + sl], AF.Identity, [f"Tn{kc % 2}", "modT"], [f"hT{kc}"],
                    bias=modT[:, l, kc, sq:sq + 1], scale=1.0)
        P.fence(["Tn0", "Tn1"], ["F1"])
        hkeys = [f"hT{kc}" for kc in range(KC)]

        def fm_group(wv, wk, c0, kcn, rhs, rkeys, n0=0, n1=None):
            n1 = NT if n1 is None else n1
            ps, pk = newps()
            for kc in range(kcn):
                mm(ps[:, n0:n1], wv[:, kc, c0:c0 + 128], rhs[:, kc, n0:n1], kc == 0, kc == kcn - 1,
                   [wk] + rkeys, [pk] if kc == 0 else [])
            return ps, pk

        def tm_group(wv, wk, off, ln, ncn):
            ps, pk = newps()
            for kc in range(KC):
                mm(ps[0:ln, 0:ncn], hT[:, kc, off:off + ln], wv[:, kc, 0:ncn], kc == 0, kc == KC - 1,
                   [wk] + hkeys, [pk] if kc == 0 else [])
            return ps, pk

        gvt = F1[:, 0:nb * 1024].rearrange("p (b n) -> p b n", b=nb)
        vn = B1[:, 0:nb * 1024].rearrange("p (b n) -> p b n", b=nb)
        p1 = B2[:, 0:8 * NTP].rearrange("p (c n) -> p c n", c=8)
        for s4 in range(4):
            wv, wk = load_w(w_in_src(l, OFF["av"] + s4 * 256), 16, 256)
            for b, (off, ln) in enumerate(blocks):
                ps, pk = tm_group(wv, wk, off, ln, 256)
                if USE_GELU_TANH:
                    act(gvt[0:ln, b, s4 * 256:(s4 + 1) * 256], ps[0:ln, 0:256], AF.Gelu_apprx_tanh, [pk], ["F1"])
                else:
                    gelu_compose(gvt[0:ln, b, s4 * 256:(s4 + 1) * 256], ps[0:ln, 0:256], pk, "F1", ln, 256)
        ln0 = blocks[0][1]
        for b, (off, ln) in enumerate(blocks):
            act(B3[0:ln, 0:1024], gvt[0:ln, b, :], AF.Square, ["F1"], ["B3", "ssq"], accum_out=ssq[0:ln, b:b + 1])
        act(rsq[0:ln0, 0:nb], ssq[0:ln0, 0:nb], AF.Ln, ["ssq"], ["rsq"], bias=EPS, scale=1.0 / 1024)
        act(rsq[0:ln0, 0:nb], rsq[0:ln0, 0:nb], AF.Exp, ["rsq"], ["rsq"], scale=-0.5)
        for b, (off, ln) in enumerate(blocks):
            if is_s:
                stt(gvt[0:ln, b, :], gvt[0:ln, b, :], rsq[0:ln, b:b + 1], vnb_sb[0:ln, :], ALU.mult, ALU.mult,
                    ["F1", "rsq", "vnb_sb"], ["F1"])
                vcopy(vn[0:ln, b, :], gvt[0:ln, b, :], ["F1"], ["B1"])
                dma("sp", vs_out[l, b], gvt[0:ln, b, :], "ostv", ["F1"], [])
            else:
                stt(vn[0:ln, b, :], gvt[0:ln, b, :], rsq[0:ln, b:b + 1], vnb_sb[0:ln, :], ALU.mult, ALU.mult,
                    ["F1", "rsq", "vnb_sb"], ["B1"])
        tAs = [F2[:, 0:512], F2[:, 1024:1536]]
        tBs = [F2[:, 512:1024], F2[:, 1536:2048]]
        kAs = ["F2a", "F2c"]
        kBs = ["F2b", "F2d"]
        for s4 in range(4):
            wvu, wku = load_w(w_in_src(l, OFF["au"] + s4 * 256), 16, 256)
            wvg, wkg = load_w(w_in_src(l, OFF["ag"] + s4 * 256), 16, 256)
            psus = [fm_group(wvu, wku, fc * 128, 16, hT, hkeys) for fc in range(2)]
            psgs = [fm_group(wvg, wkg, fc * 128, 16, hT, hkeys) for fc in range(2)]
            for fc in range(2):
                act(tAs[fc][:, 0:NT], psus[fc][0][:, 0:NT], AF.Gelu_apprx_tanh, [psus[fc][1]], [kAs[fc]])
            for fc in range(2):
                act(tBs[fc][:, 0:NT], psgs[fc][0][:, 0:NT], AF.Silu, [psgs[fc][1]], [kBs[fc]])
            for fc in range(2):
                c = s4 * 2 + fc
                tt(p1[:, c, 0:NT], tAs[fc][:, 0:NT], tBs[fc][:, 0:NT], ALU.mult, [kAs[fc], kBs[fc]], ["B2"])
        for g in range(8):
            ps, pk = newps()
            for b, (off, ln) in enumerate(blocks):
                mm(ps[:, off:off + ln], vn[0:ln, b, g * 128:(g + 1) * 128], gmw_bf[0:ln, g * 128:g * 128 + ln], True, False,
                   ["B1", "gmw_bf"], [pk] if b == 0 else [])
                mm(ps[:, off:off + ln], ccol("ones", 0, 128)[0:1, :], gmb_sb[0:1, g * 128:g * 128 + ln], False, True,
                   ["const", "gmb_sb"], [], touch=[pk])
            tt(yA[:, g, 0:NT], p1[:, g, 0:NT], ps[:, 0:NT], ALU.mult, ["B2", pk], ["yA"])

        NTH = NT + 3 * len(segs)
        P.fence(["F1", "F2a", "F2b", "F2c", "F2d"], [f"TB{i}" for i in range(10)])
        P.fence(["B1"], ["slg0", "slg1"])
        TB = [F1[:, i * 528:(i + 1) * 528] for i in range(8)] + [F2[:, 1024:1536], F2[:, 1536:2048]]
        for s4 in range(4):
            wvx, wkx = load_w(w_in_src(l, OFF["lx"] + s4 * 256), 16, 256)
            wvg, wkg = load_w(w_in_src(l, OFF["lg"] + s4 * 256), 16, 256)
            psx_l = [fm_group(wvx, wkx, fc * 128, 16, hT, hkeys) for fc in range(2)]
            psg_l = [fm_group(wvg, wkg, fc * 128, 16, hT, hkeys) for fc in range(2)]

            def chunk_gen(fc, s4=s4, psx_l=psx_l, psg_l=psg_l):
                c = s4 * 2 + fc
                par = (c % 2) * 5
                lxh, kx = TB[par + 0], f"TB{par + 0}"
                xc, kxc = TB[par + 1], f"TB{par + 1}"
                tr, ktr = TB[par + 2], f"TB{par + 2}"
                ti, kti = TB[par + 3], f"TB{par + 3}"
                ta, kta = TB[par + 4], f"TB{par + 4}"
                xcb = B3[:, (c % 2) * 512:(c % 2) * 512 + 512]
                kxb = f"B3x{c % 2}"
                slg = B1[:, (c % 2) * 512:(c % 2) * 512 + 512]
                kslg = f"slg{c % 2}"
                psx, pkx = psx_l[fc]
                psg, pkg = psg_l[fc]
                lxv = lxh[:, 0:NTH].rearrange("p (s n) -> p s n", s=len(segs))
                for si, (so, sl, sq) in enumerate(segs):
                    if is_s:
                        o = VL["convs"][0] + ((l * 8 + c) * 4 + si) * 3
                        vcopy(lxv[:, si, 0:3], vec_sb[:, o:o + 3], ["vec"], [kx])
                    else:
                        vcopy(lxv[:, si, 0:3], convP[:, l, c, :], ["convP"], [kx])
                    act(lxv[:, si, 3:3 + sl], psx[:, so:so + sl], AF.Copy, [pkx], [kx])
                act(slg[:, 0:NT], psg[:, 0:NT], AF.Silu, [pkg], [kslg])
                for si, (so, sl, sq) in enumerate(segs):
                    if is_s:
                        o = SL["convs"][0] + ((l * 8 + c) * 4 + si) * 3
                        vcopy(so_sb[:, o:o + 3], lxv[:, si, sl:sl + 3], [kx], ["so_sb"])
                    else:
                        vcopy(convP[:, l, c, :], lxv[:, si, sl:sl + 3], [kx], ["convP"])
                yield
                sl = segs[0][1]
                xcv = xc[:, 0:NT].rearrange("p (s n) -> p s n", s=len(segs))
                cw = lambda j: vcol("convw", (l * 4 + j) * 8 + c)
                ts(xcv, lxv[:, :, 0:sl], cw(0), vcol("convb", l * 8 + c), ALU.mult, ALU.add, [kx, "vec"], [kxc])
                for j in range(1, 4):
                    stt(xcv, lxv[:, :, j:j + sl], cw(j), xcv, ALU.mult, ALU.add, [kx, kxc, "vec"], [kxc])
                vcopy(xcb[:, 0:NT], xc[:, 0:NT], [kxc], [kxb])
                yield
                psr, pkr = newps()
                mm(psr[:, 0:NT], wa_bf[:, c * 128:(c + 1) * 128], xcb[:, 0:NT], True, True, ["wa_bf", kxb], [pkr])
                psi, pki = newps()
                mm(psi[:, 0:NT], wx_bf[:, c * 128:(c + 1) * 128], xcb[:, 0:NT], True, True, ["wx_bf", kxb], [pki])
                act(tr[:, 0:NT], psr[:, 0:NT], AF.Sigmoid, [pkr, "vec"], [ktr], bias=vcol("ba", l * 8 + c), scale=1.0)
                act(ti[:, 0:NT], psi[:, 0:NT], AF.Sigmoid, [pki, "vec"], [kti], bias=vcol("bx", l * 8 + c), scale=1.0)
                yield
                act(ta[:, 0:NT], tr[:, 0:NT], AF.Exp, [ktr, "s1"], [kta], scale=s1[:, l, c:c + 1])
                act(lxh[:, 0:NT], tr[:, 0:NT], AF.Tanh, [ktr, "s1"], [kx], scale=s1[:, l, c:c + 1])
                act(tr[:, 0:NT], tr[:, 0:NT], AF.Exp, [ktr, "s2"], [ktr], scale=s2[:, l, c:c + 1])
                yield
                stt(tr[:, 0:NT], tr[:, 0:NT], 1.0, lxh[:, 0:NT], ALU.add, ALU.mult, [ktr, kx], [ktr])
                tt(ti[:, 0:NT], ti[:, 0:NT], xc[:, 0:NT], ALU.mult, [kti, kxc], [kti])
                yield
                act(tr[:, 0:NT], tr[:, 0:NT], AF.Ln, [ktr], [ktr], scale=-1.0)
                act(tr[:, 0:NT], tr[:, 0:NT], AF.Exp, [ktr], [ktr], scale=0.5)
                yield
                tt(ti[:, 0:NT], ti[:, 0:NT], tr[:, 0:NT], ALU.mult, [kti, ktr], [kti])
                for si, (so, sl, sq) in enumerate(segs):
                    if is_s:
                        o = VL["hs"][0] + (l * 8 + c) * 4 + si
                        init = vec_sb[:, o:o + 1]
                        ik = "vec"
                    else:
                        init = hP[:, l, c:c + 1]
                        ik = "hP"
                    P.op("dve", lambda e, so=so, sl=sl, init=init, tr=tr, ta=ta, ti=ti: e.tensor_tensor_scan(
                        out=tr[:, so:so + sl], data0=ta[:, so:so + sl], data1=ti[:, so:so + sl], initial=init,
                        op0=ALU.mult, op1=ALU.add), [kta, kti, ik], [ktr])
                    if is_s:
                        o2 = SL["hs"][0] + (l * 8 + c) * 4 + si
                        vcopy(so_sb[:, o2:o2 + 1], tr[:, so + sl - 1:so + sl], [ktr], ["so_sb"])
                    else:
                        vcopy(hP[:, l, c:c + 1], tr[:, so + sl - 1:so + sl], [ktr], ["hP"])
                tt(yB[:, c, 0:NT], tr[:, 0:NT], slg[:, 0:NT], ALU.mult, [ktr, kslg], ["yB"])
                yield
            gens = [chunk_gen(fc) for fc in range(2)]
            for _stage in range(7):
                for g_ in gens:
                    next(g_)
        P.fence([f"TB{i}" for i in range(10)] + ["B3x0", "B3x1", "slg0", "slg1"], ["F1", "F2", "B3", "B1"])

        chunks = tc["chunks"]
        nch = len(chunks)
        L = tc["L"]
        mid = L // 2 - 1
        rm0 = 0 if not is_s else NTP
        hblocks = tc["hblocks"]
        nhb = len(hblocks)
        for qt in range(4):
            G = F1[:, 0:1024].rearrange("p (h n) -> p h n", h=2)
            Qf = F1[:, 1024:2048].rearrange("p (h n) -> p h n", h=2)
            E = F1[:, 2048:3072].rearrange("p (h n) -> p h n", h=2)
            KtT = B1[:, 0:1024].rearrange("p (h n) -> p h n", h=2)
            KhT = B1[:, 1024:2048].rearrange("p (h n) -> p h n", h=2)
            QtT = B1[:, 2048:3072].rearrange("p (h n) -> p h n", h=2)
            Vt = B2[:, 0:nhb * 256].rearrange("p (b n) -> p b n", b=nhb)
            Ktok = B2[:, 2048:2048 + nhb * 256].rearrange("p (b h n) -> p b h n", b=nhb, h=2)
            shg = B3[:, 0:1024].rearrange("p (h n) -> p h n", h=2)
            AmA = B3[:, 1024:2048].rearrange("p (h n) -> p h n", h=2)
            Sb = B3[:, 2048:2048 + 1024].rearrange("p (i n) -> p i n", i=8)
            otk = F2[:, 0:nhb * 256].rearrange("p (b h n) -> p b h n", b=nhb, h=2)
            wv, wk = load_w(w_in_src(l, OFF["hf"] + qt * 256), 16, 256)
            for fc in range(2):
                hh = fc
                hd = qt * 2 + hh
                ps, pk = fm_group(wv, wk, fc * 128, 16, hT, hkeys)
                act(E[:, hh, 0:NT], ps[:, 0:NT], AF.Sigmoid, [pk], [f"E{hh}"])
                ts(G[:, hh, 0:NT], E[:, hh, 0:NT], oml[:, l, hd:hd + 1], lbs[:, l, hd:hd + 1], ALU.mult, ALU.add,
                   [f"E{hh}", "oml", "lbs"], [f"G{hh}"])
                ts(KtT[:, hh, 0:NT], G[:, hh, 0:NT], -1.0, 1.0, ALU.mult, ALU.add, [f"G{hh}"], [f"Kt{hh}"])
                act(E[:, hh, 0:NT], G[:, hh, 0:NT], AF.Ln, [f"G{hh}"], [f"E{hh}"])
                P.op("dve", lambda e, hh=hh, G=G, E=E, rm0=rm0, NT=NT: e.tensor_tensor_scan(
                    out=G[:, hh, 0:NT], data0=rmask_bf[:, rm0:rm0 + NT], data1=E[:, hh, 0:NT], initial=0.0,
                    op0=ALU.mult, op1=ALU.add), [f"E{hh}", "rmask"], [f"G{hh}"])
            wv, wk = load_w(w_in_src(l, OFF["hq"] + qt * 256), 16, 256)
            for fc in range(2):
                hh = fc
                ps, pk = fm_group(wv, wk, fc * 128, 16, hT, hkeys)
                act(Qf[:, hh, 0:NT], ps[:, 0:NT], AF.Silu, [pk], [f"Qf{hh}"])
            wv, wk = load_w(w_in_src(l, OFF["hi"] + qt * 256), 16, 256)
            for b, (off, ln) in enumerate(hblocks):
                ps, pk = tm_group(wv, wk, off, ln, 256)
                act(Vt[0:ln, b, :], ps[0:ln, 0:256], AF.Copy, [pk], ["Vt"])
            wv, wk = load_w(w_in_src(l, OFF["hg"] + qt * 256), 16, 256)
            for fc in range(2):
                hh = fc
                ps, pk = fm_group(wv, wk, fc * 128, 16, hT, hkeys)
                act(shg[:, hh, 0:NT], ps[:, 0:NT], AF.Silu, [pk], [f"shg{hh}"])
            def head_gen(hh):
                Gc = G[:, hh, 0:NT].rearrange("p (c n) -> p c n", n=L)
                Ev = E[:, hh, 0:NT].rearrange("p (c n) -> p c n", n=L)

                def rescan():
                    P.op("dve", lambda e, hh=hh, G=G, E=E, rm0=rm0, NT=NT: e.tensor_tensor_scan(
                        out=G[:, hh, 0:NT], data0=rmask_bf[:, rm0:rm0 + NT], data1=E[:, hh, 0:NT], initial=0.0,
                        op0=ALU.mult, op1=ALU.add), [f"E{hh}", "rmask"], [f"G{hh}"])
                vcopy(hgs[:, 1, hh, 0:nch], Gc[:, :, mid], [f"G{hh}"], [f"hgs{hh}"])
                vcopy(hgs[:, 4, hh, 0:nch], Gc[:, :, L - 1], [f"G{hh}"], [f"hgs{hh}"])
                act(hgs[:, 2, hh, 0:nch], Gc[:, :, mid], AF.Exp, [f"G{hh}"], [f"hgs{hh}"])
                act(hgs[:, 3, hh, 0:nch], Gc[:, :, L - 1], AF.Exp, [f"G{hh}"], [f"hgs{hh}"])
                tt(hgs[:, 0, hh, 0:nch], hgs[:, 4, hh, 0:nch], hgs[:, 1, hh, 0:nch], ALU.subtract, [f"hgs{hh}"], [f"hgs{hh}"])
                tt(Ev[:, :, 0], Ev[:, :, 0], hgs[:, 4, hh, 0:nch], ALU.subtract, [f"E{hh}", f"hgs{hh}"], [f"E{hh}"])
                rescan()
                yield
                act(G[:, hh, 0:NT], G[:, hh, 0:NT], AF.Exp, [f"G{hh}"], [f"G{hh}"], scale=-1.0)
                yield
                tt(KhT[:, hh, 0:NT], KtT[:, hh, 0:NT], G[:, hh, 0:NT], ALU.mult, [f"Kt{hh}", f"G{hh}"], [f"Kh{hh}"])
                tt(Ev[:, :, 0], Ev[:, :, 0], hgs[:, 0, hh, 0:nch], ALU.add, [f"E{hh}", f"hgs{hh}"], [f"E{hh}"])
                rescan()
                yield
                act(E[:, hh, 0:NT], G[:, hh, 0:NT], AF.Exp, [f"G{hh}"], [f"E{hh}"], scale=-1.0)
                act(G[:, hh, 0:NT], G[:, hh, 0:NT], AF.Exp, [f"G{hh}"], [f"G{hh}"])
                for b, (off, ln) in enumerate(hblocks):
                    ps2, pk2 = newps()
                    pb = ps2[:].bitcast(BF16)
                    transpose(pb[0:ln, 0:128], KhT[:, hh, off:off + ln], ident_bf[:], [f"Kh{hh}", "ident_bf"], [pk2])
                    vcopy(Ktok[0:ln, b, hh, :], pb[0:ln, 0:128], [pk2], [f"Ktok{hh}"])
                yield
                tt(KtT[:, hh, 0:NT], KtT[:, hh, 0:NT], E[:, hh, 0:NT], ALU.mult, [f"Kt{hh}", f"E{hh}"], [f"Kt{hh}"])
                tt(QtT[:, hh, 0:NT], Qf[:, hh, 0:NT], G[:, hh, 0:NT], ALU.mult, [f"Qf{hh}", f"G{hh}"], [f"Qt{hh}"])
                yield
            hgens = [head_gen(hh) for hh in range(2)]
            for _stage in range(5):
                for g_ in hgens:
                    next(g_)
            R = 32 if is_s else 64
            maskA = (maskS_u32 if is_s else maskP_u32)
            for hh in range(2):
                psA, pkA = newps()
                for ch in chunks:
                    ci, co, p0, ln = ch["ci"], ch["off"], ch["p0"], ch["ln"]
                    mm(psA[p0:p0 + ln, ci * 32:ci * 32 + ln], KtT[:, hh, co:co + ln], QtT[:, hh, co:co + ln], True, True,
                       [f"Kt{hh}", f"Qt{hh}"], [pkA] if ci == 0 else [], touch=[pkA])
                P.op("dve", lambda e, AmA=AmA, psA=psA, hh=hh, R=R, nch=nch, maskA=maskA: e.copy_predicated(
                    out=AmA[0:R, hh, 0:nch * 32], mask=maskA[0:R, 0:nch * 32], data=psA[0:R, 0:nch * 32]),
                    [pkA, "const"], [f"AmA{hh}"])
            nob = (nhb + 3) // 4
            pso_b = [[newps() for _ in range(nob)] for hh in range(2)]
            pss_b = [newps() for _ in range(2)]
            units = [(ch, hh) for ch in chunks for hh in range(2)]
            ngrp = (len(units) + 3) // 4
            pss_q = {}

            def emit_pss_group(g):
                bank, bk = pss_b[g % 2]
                for j in range(4):
                    ui = g * 4 + j
                    if ui >= len(units):
                        break
                    ch, hh = units[ui]
                    p0, ln, b = ch["p0"], ch["ln"], ch["blk"]
                    mm(bank[:, j * 128:(j + 1) * 128], Ktok[p0:p0 + ln, b, hh, :], Vt[p0:p0 + ln, b, hh * 128:(hh + 1) * 128], True, True,
                       [f"Ktok{hh}", "Vt"], [bk], touch=[bk])
                    pss_q[ui] = (bank[:, j * 128:(j + 1) * 128], bk)
            for g in range(min(2, ngrp)):
                emit_pss_group(g)
            sb_of = {}
            sbi = 0

            def emit_sb(ch, hh):
                nonlocal_sbi = sb_cnt[0]
                sb_cnt[0] += 1
                hd = qt * 2 + hh
                Sh = Sst[:, l, hd * 128:(hd + 1) * 128]
                buf = Sb[:, nonlocal_sbi % 8, :]
                key = f"Sb{nonlocal_sbi % 8}"
                ts(buf, Sh, hgs[:, 2, hh, ch["ci"]:ch["ci"] + 1], None, ALU.mult, None, [f"S{l}_{hd}", f"hgs{hh}"], [key])
                sb_of[(ch["ci"], hh)] = (buf, key)
            sb_cnt = [0]
            if not is_s:
                for hh in range(2):
                    emit_sb(chunks[0], hh)
            first_pso = {}
            for ui, (ch, hh) in enumerate(units):
                ci, co, p0, ln, b, si = ch["ci"], ch["off"], ch["p0"], ch["ln"], ch["blk"], ch["seg"]
                hd = qt * 2 + hh
                Sh = Sst[:, l, hd * 128:(hd + 1) * 128]
                skey = f"S{l}_{hd}"
                if is_s:
                    dma("sp", Sh, Ssin[l, si, :, hd * 128:(hd + 1) * 128], f"sld{hd}", [], [skey])
                    emit_sb(ch, hh)
                sbuf_, sbk = sb_of[(ci, hh)]
                pso, pko = pso_b[hh][b // 4]
                oc = (b % 4) * 128
                fk = (hh, b // 4)
                mm(pso[p0:p0 + ln, oc:oc + 128], QtT[:, hh, co:co + ln], sbuf_, True, False, [f"Qt{hh}", sbk],
                   [pko])
                first_pso[fk] = True
                mm(pso[p0:p0 + ln, oc:oc + 128], AmA[p0:p0 + ln, hh, ci * 32:ci * 32 + ln], Vt[p0:p0 + ln, b, hh * 128:(hh + 1) * 128],
                   False, True, [f"AmA{hh}", "Vt"], [], touch=[pko])
                pss, pssk = pss_q.pop(ui)
                stt(Sh, Sh, hgs[:, 3, hh, ci:ci + 1], pss, ALU.mult, ALU.add, [skey, f"hgs{hh}", pssk], [skey])
                if is_s:
                    dma("sp", Ss_out[l, si, :, hd * 128:(hd + 1) * 128], Sh, f"ostS{hd}", [skey], [])
                else:
                    if ci + 1 < nch:
                        emit_sb(chunks[ci + 1], hh)
                    elif last_tile:
                        dma("sp", Sp_out[l, :, hd * 128:(hd + 1) * 128], Sh, f"ostS{hd}", [skey], [])
                if ui % 4 == 3 and ui // 4 + 2 < ngrp:
                    emit_pss_group(ui // 4 + 2)
            for hh in range(2):
                for ob in range(nob):
                    pso, pko = pso_b[hh][ob]
                    nbl = min(4, nhb - ob * 4)
                    act(otk[0:R, ob * 4:ob * 4 + nbl, hh, :], pso[0:R, 0:nbl * 128].rearrange("p (b n) -> p b n", b=nbl), AF.Copy,
                        [pko], [f"otk{b_}" for b_ in range(ob * 4, ob * 4 + nbl)])
            onb = const_sb[:, CL["onormb"][0] + l * 128:CL["onormb"][0] + (l + 1) * 128]
            for b, (off, ln) in enumerate(hblocks):
                for hh in range(2):
                    act(B3[0:ln, 3584:3584 + 128], otk[0:ln, b, hh, :], AF.Square, [f"otk{b}"], ["junk", "ssq"],
                        accum_out=ssq[0:ln, hh:hh + 1])
                act(rsq[0:ln, 0:2], ssq[0:ln, 0:2], AF.Ln, ["ssq"], ["rsq"], bias=EPS, scale=1.0 / 128)
                act(rsq[0:ln, 0:2], rsq[0:ln, 0:2], AF.Exp, ["rsq"], ["rsq"], scale=-0.5)
                for hh in range(2):
                    hd = qt * 2 + hh
                    onk = f"on{hh % 2}"
                    on = B3[:, 3328 + (hh % 2) * 128:3328 + (hh % 2) * 128 + 128]
                    stt(on[0:ln, :], otk[0:ln, b, hh, :], rsq[0:ln, hh:hh + 1], onb[0:ln, :], ALU.mult, ALU.mult,
                        [f"otk{b}", "rsq", "const"], [onk])
                    ps2, pk2 = newps()
                    pb = ps2[:].bitcast(BF16)
                    transpose(pb[:, 0:ln], on[0:ln, :], ident_bf[0:ln, 0:ln], [onk, "ident_bf"], [pk2])
                    tt(yC[:, hd, off:off + ln], pb[:, 0:ln], shg[:, hh, off:off + ln], ALU.mult, [pk2, f"shg{hh}"], ["yC"])
            allk = [f"G{h}" for h in range(4)] + [f"E{h}" for h in range(4)] + [f"Kt{h}" for h in range(4)] + \
                   [f"Qt{h}" for h in range(4)] + [f"Ktok{h}" for h in range(4)] + [f"shg{h}" for h in range(4)] + \
                   [f"Kh{h}" for h in range(2)] + [f"Qf{h}" for h in range(2)] + [f"AmA{h}" for h in range(2)] + \
                   ["Vt", "junk", "on0", "on1", "tmpS0", "tmpS1"] + [f"Sb{i}" for i in range(8)] + [f"Am{i}" for i in range(4)] + \
                   [f"otk{b}" for b in range(8)]
            P.fence(allk, allk + ["F1", "F2", "B1", "B2", "B3"])
        allk = [f"G{h}" for h in range(4)] + [f"E{h}" for h in range(4)] + [f"Kt{h}" for h in range(4)] + \
               [f"Qt{h}" for h in range(4)] + [f"Ktok{h}" for h in range(4)] + [f"shg{h}" for h in range(4)] + \
               [f"Kh{h}" for h in range(2)] + [f"Qf{h}" for h in range(2)] + [f"AmA{h}" for h in range(2)] + \
               ["Vt", "junk", "on0", "on1", "tmpS0", "tmpS1"] + [f"Sb{i}" for i in range(8)] + [f"Am{i}" for i in range(4)] + \
               [f"otk{b}" for b in range(8)]
        if debug and l == 0 and (is_s or tile_index[0] == 0):
            di = 1 if is_s else 0
            for bi, (yy, yk) in enumerate(((yA, "yA"), (yB, "yB"), (yC, "yC"))):
                dma("pool", dbg[di, bi], yy[:].rearrange("p c n -> p (c n)"), f"dbg{bi}", [yk], [])
        mkeys = [f"mT{j}" for j in range(16)]
        tkeys = [f"TM{i}" for i in range(4)] + [f"ACC{i}" for i in range(4)]
        P.fence(allk + ["F1", "F2", "B1", "B2", "B3"], mkeys + tkeys)

        mT = F1[:].bitcast(BF16)[:, 0:16 * NTP].rearrange("p (k n) -> p k n", k=16)
        ACC = [F2[:, i * 512:(i + 1) * 512] for i in range(4)]
        B2f = B2[:].bitcast(F32)
        TM = [B2f[:, i * 512:(i + 1) * 512] for i in range(4)]
        ys = [(yA, "yA"), (yB, "yB"), (yC, "yC")]
        tmi = 0
        for j4 in range(4):
            for br in range(3):
                wbv, wbk = load_w(w_br[l, br * 1024:(br + 1) * 1024, j4 * 512:(j4 + 1) * 512].rearrange("(k p) n -> p k n", p=128), 8, 512)
                pps = [fm_group(wbv, wbk, jj_ * 128, 8, ys[br][0], [ys[br][1]]) for jj_ in range(4)]
                for h2 in range(2):
                    wgv, wgk = load_w(w_in_src(l, OFF["mg"] + br * 2048 + j4 * 512 + h2 * 256), 16, 256)
                    for fc in range(2):
                        jj = h2 * 2 + fc
                        j = j4 * 4 + jj
                        pg, pgk = fm_group(wgv, wgk, fc * 128, 16, hT, hkeys)
                        pp, ppk = pps[jj]
                        t = TM[tmi % 4]
                        tk = f"TM{tmi % 4}"
                        tmi += 1
                        act(t[:, 0:NT], pg[:, 0:NT], AF.Sigmoid, [pgk], [tk])
                        if br == 0:
                            tt(ACC[jj][:, 0:NT], t[:, 0:NT], pp[:, 0:NT], ALU.mult, [tk, ppk], [f"ACC{jj}"])
                        else:
                            tt(t[:, 0:NT], t[:, 0:NT], pp[:, 0:NT], ALU.mult, [tk, ppk], [tk])
                            if br == 1:
                                tt(ACC[jj][:, 0:NT], ACC[jj][:, 0:NT], t[:, 0:NT], ALU.add, [tk, f"ACC{jj}"], [f"ACC{jj}"])
                            else:
                                tt(mT[:, j, 0:NT], ACC[jj][:, 0:NT], t[:, 0:NT], ALU.add, [tk, f"ACC{jj}"], [f"mT{j}"])
        if debug and l == 0 and (is_s or tile_index[0] == 0):
            di = 1 if is_s else 0
            dma("pool", dbg[di, 3], mT[:, 0:8, :].rearrange("p c n -> p (c n)"), "dbg3", mkeys, [])
        for j2 in range(8):
            wv, wk = load_w(w_out[l, :, j2 * 256:(j2 + 1) * 256].rearrange("(k p) n -> p k n", p=128), 16, 256)
            for fc in range(2):
                j = j2 * 2 + fc
                ps, pk = fm_group(wv, wk, fc * 128, 16, mT, mkeys)
                for (so, sl, sq) in segs:
                    stt(xT[:, j, so:so + sl], ps[:, so:so + sl], modT[:, l, 32 + j, sq:sq + 1], xT[:, j, so:so + sl],
                        ALU.mult, ALU.add, [pk, "modT", f"xT{j}"], [f"xT{j}"])
        P.fence(mkeys + tkeys, ["F1", "F2", "B1", "B2", "B3"])

    def gelu_compose(out, ps, pk, okey, npart, n):
        raise NotImplementedError

    tiles = [("p", i) for i in range(n_pt)] + [("s", 0)]
    tile_index = [0]
    for (kind, ti) in tiles:
        tc = tile_cfg(kind)
        NT = tc["NT"]
        if kind == "p":
            src = xpT[:, ti * NTP:(ti + 1) * NTP]
            dst = yT[:, ti * NTP:(ti + 1) * NTP]
        else:
            src = xsT[:, :]
            dst = yT[:, SEQL:SEQL + NSMP]
        xkeys = [f"xT{kc}" for kc in range(KC)]
        dma("sp", xT[:, :, 0:NT], src.rearrange("(k p) n -> p k n", p=128), "xld", [], xkeys)
        tile_index[0] = ti
        for l in range(depth):
            layer(tc, l, last_tile=(kind == "p" and ti == n_pt - 1))
        rms_stats(NT)
        for kc in range(KC):
            stt(xT[:, kc, 0:NT], xT[:, kc, 0:NT], vcol("finalg", kc), rstd[:, 0:NT], ALU.mult, ALU.mult,
                [f"xT{kc}", "vec", "rstd"], [f"xT{kc}"])
        dma("sp", dst.rearrange("(k p) n -> p k n", p=128), xT[:, :, 0:NT], "osty", xkeys, [])
        if kind == "p" and ti == n_pt - 1:
            o = SL["convp"][0]
            vcopy(so_sb[:, o:o + depth * 24], convP[:].rearrange("p l c t -> p (l c t)"), ["convP"], ["so_sb"])
            o = SL["hp"][0]
            vcopy(so_sb[:, o:o + depth * 8], hP[:].rearrange("p l c -> p (l c)"), ["hP"], ["so_sb"])
    dma("sp", small_out[:, :], so_sb[:], "osts", ["so_sb"], [])
    P.final_wait("sp", [x for x in P.dmasems if x.startswith("ost") or x.startswith("dbg")])

    sem_names = list(ENGS) + P.dmasems
    sems = {n: es.enter_context(nc.semaphore(n)) for n in sem_names}
    block = es.enter_context(nc.Block())

    def run(engh, items):
        for it in items:
            if it[0] == "w":
                engh.wait_ge(sems[it[1]], it[2])
            else:
                inst = it[1](engh)
                if it[2] is not None:
                    inst.then_inc(sems[it[2]], it[3])

    @block.tensor
    def _(e):
        run(e, P.lists["pe"])

    @block.scalar
    def _(e):
        run(e, P.lists["act"])

    @block.vector
    def _(e):
        run(e, P.lists["dve"])

    @block.gpsimd
    def _(e):
        run(e, P.lists["pool"])

    @block.sync
    def _(e):
        run(e, P.lists["sp"])

    es.close()
    return nc


def pmajor(a):
    a = np.asarray(a, np.float32)
    lead = a.shape[:-1]
    n = a.shape[-1] // 128
    a = a.reshape(*lead, n, 128)
    a = np.moveaxis(a, -1, 0)
    return np.ascontiguousarray(a)


def prep_core(inp, core, depth, seql):
    VL, NV = vec_layout(depth)
    CL, NCONST = const_layout(depth)
    ps = core % 4
    ss = slice(4 * core, 4 * core + 4)
    vec = np.zeros((128, NV), np.float32)

    def put(name, arr):
        o, n = VL[name]
        vec[:, o:o + n] = arr.reshape(128, n)

    cc = np.concatenate([inp["c_prompt"][ps:ps + 1], inp["c_sample"][ss]], axis=0)
    put("c", np.moveaxis(pmajor(cc), 1, 2))
    put("bada", pmajor(inp["b_ada"][:depth]))
    put("normg", pmajor(inp["norm_g"][:depth]))
    put("finalg", pmajor(inp["final_g"]))
    put("ba", pmajor(inp["lru_ba"][:depth]))
    put("bx", pmajor(inp["lru_bx"][:depth]))
    put("lam", pmajor(inp["lru_lambda"][:depth]))
    put("convb", pmajor(inp["lru_conv_b"][:depth]))
    put("convw", pmajor(inp["lru_conv_w"][:depth]))
    put("hglb", pmajor(inp["hg_lb"][:depth]))
    cs = inp["state_rglru_conv"][:depth, ss]
    cs = pmajor(cs)
    put("convs", np.transpose(cs, (0, 1, 4, 2, 3)))
    hs = pmajor(inp["state_rglru_h"][:depth, ss])
    put("hs", np.transpose(hs, (0, 1, 3, 2)))

    m = {}
    m["xpT"] = np.ascontiguousarray(inp["x_prompt"][ps, :seql].T)
    m["xsT"] = np.ascontiguousarray(inp["x_sample"][ss].reshape(NSMP, D).T)
    m["vecs"] = vec
    S = inp["state_hgrn2"][:depth, ss]
    m["Ssin"] = np.ascontiguousarray(np.transpose(S, (0, 1, 3, 2, 4)).reshape(depth, 4, 128, 1024))
    return m


def prep_shared(inp, depth):
    CL, NCONST = const_layout(depth)
    con = np.zeros((128, NCONST), np.float32)

    def putc(name, arr):
        o, n = CL[name]
        con[:, o:o + n] = arr
    putc("ident", np.eye(128, dtype=np.float32))
    putc("triu", np.triu(np.ones((128, 128), np.float32)))
    rp = np.ones(512, np.float32)
    rp[::32] = 0
    putc("rmask_p", np.broadcast_to(rp, (128, 512)))
    rs = np.ones(128, np.float32)
    rs[::32] = 0
    putc("rmask_s", np.broadcast_to(rs, (128, 128)))
    putc("onormb", np.broadcast_to(inp["hg_onorm_g"][:depth].reshape(1, depth * 128), (128, depth * 128)))
    putc("ones", np.ones((128, 128), np.float32))
    t32 = np.triu(np.ones((32, 32), np.float32))
    mp = np.zeros((128, 512), np.float32)
    for ci in range(16):
        p0 = (ci % 2) * 32
        mp[p0:p0 + 32, ci * 32:(ci + 1) * 32] = t32
    putc("maskP", mp)
    ms = np.zeros((128, 128), np.float32)
    for ci in range(4):
        ms[0:32, ci * 32:(ci + 1) * 32] = t32
    putc("maskS", ms)
    m = {"consts": con}
    m["w_ada"] = np.ascontiguousarray(inp["w_ada"][:depth])
    m["w_in"] = np.ascontiguousarray(inp["w_in"][:depth])
    m["w_branch"] = np.ascontiguousarray(inp["w_branch"][:depth])
    m["w_out"] = np.ascontiguousarray(inp["w_out"][:depth])
    m["gmw"] = np.ascontiguousarray(np.transpose(inp["gm_ws"][:depth], (0, 3, 1, 2)).reshape(depth, 128, 1024))
    m["gmb"] = np.ascontiguousarray(inp["gm_bs"][:depth].reshape(depth, 1, 1024))
    m["vnb"] = np.ascontiguousarray(np.broadcast_to(inp["gm_vnorm_g"][:depth, None, :], (depth, 128, 1024)))
    m["lwa"] = np.ascontiguousarray(np.transpose(inp["lru_wa"][:depth], (0, 2, 1, 3)).reshape(depth, 128, 1024))
    m["lwx"] = np.ascontiguousarray(np.transpose(inp["lru_wx"][:depth], (0, 2, 1, 3)).reshape(depth, 128, 1024))
    return m


def unpmajor(a):
    a = np.moveaxis(a, 0, -1)
    return np.ascontiguousarray(a).reshape(*a.shape[:-2], a.shape[-2] * 128)


def assemble(results, depth, seql, ncores):
    SL, NSO = so_layout(depth)
    nseq_p = min(4, ncores)
    y_prompt = np.zeros((nseq_p, seql, D), np.float32)
    y_sample = np.zeros((4 * ncores, 32, D), np.float32)
    conv_p = np.zeros((depth, nseq_p, 3, 1024), np.float32)
    h_p = np.zeros((depth, nseq_p, 1024), np.float32)
    S_p = np.zeros((depth, nseq_p, 8, 128, 128), np.float32)
    conv_s = np.zeros((depth, 4 * ncores, 3, 1024), np.float32)
    h_s = np.zeros((depth, 4 * ncores, 1024), np.float32)
    S_s = np.zeros((depth, 4 * ncores, 8, 128, 128), np.float32)
    v_s = np.zeros((depth, 4 * ncores, 32, 1024), np.float32)
    for c, r in enumerate(results):
        yT = r["yT"]
        so = r["small_out"]

        def get(name, shape):
            o, n = SL[name]
            return so[:, o:o + n].reshape(128, *shape)
        if c < 4:
            y_prompt[c] = yT[:, :seql].T
            cp = get("convp", (depth, 8, 3))
            conv_p[:, c] = np.transpose(unpmajor(np.transpose(cp, (0, 1, 3, 2))), (0, 1, 2))
            h_p[:, c] = unpmajor(get("hp", (depth, 8)))
            S_p[:, c] = np.transpose(r["Sp_out"].reshape(depth, 128, 8, 128), (0, 2, 1, 3))
        ss = slice(4 * c, 4 * c + 4)
        y_sample[ss] = yT[:, seql:].T.reshape(4, 32, D)
        cs = get("convs", (depth, 8, 4, 3))
        conv_s[:, ss] = unpmajor(np.transpose(cs, (0, 1, 3, 4, 2)))
        hs = get("hs", (depth, 8, 4))
        h_s[:, ss] = unpmajor(np.transpose(hs, (0, 1, 3, 2)))
        S_s[:, ss] = np.transpose(r["Ss_out"].reshape(depth, 4, 128, 8, 128), (0, 1, 3, 2, 4))
        v_s[:, ss] = r["vs_out"]
    return (y_prompt, y_sample, conv_p, h_p, S_p, conv_s, h_s, S_s, v_s)


_NC_CACHE = {}


def run(inp, depth=DEPTH, n_pt=SEQ // NTP, ncores=8, trace=False):
    key = (depth, n_pt)
    if key not in _NC_CACHE:
        _NC_CACHE[key] = build(depth, n_pt)
    nc = _NC_CACHE[key]
    seql = n_pt * NTP
    shared = prep_shared(inp, depth)
    in_maps = []
    for c in range(ncores):
        m = dict(shared)
        m.update(prep_core(inp, c, depth, seql))
        in_maps.append(m)
    res = run_bass_kernel_spmd(nc, in_maps, core_ids=list(range(ncores)), **({"trace": True} if trace else {}))
    return assemble(res.results, depth, seql, ncores), res


def kernel(**inputs):
    inp = {k: np.asarray(v) for k, v in inputs.items()}
    outs, _ = run(inp)
    return outs
```
